# Optimizing a Trainium2 kernel written in Bass

```python
import jax, jax.numpy as jnp
from jax import lax
import numpy as np

D_MODEL = 1024
BATCH = 16
SEQ = 2048
DEPTH = 1

HEAD_DIM = 64
N_HEADS_A = 8
N_HEADS_B = 8
D_A = N_HEADS_A * HEAD_DIM
D_B = N_HEADS_B * HEAD_DIM
D_MIX = D_A + D_B
DILATED_PATTERNS = ((128, 1), (512, 4), (2048, 16))
MOBA_BLOCK = 256
MOBA_TOPK = 3
MOBA_Q_CHUNK = 128
ROPE_THETA = 10000.0
LN_EPS = 1e-5
NEG_INF = -1e30
DEEPNORM_ALPHA = (2.0 * DEPTH) ** 0.25
DEEPNORM_BETA = (8.0 * DEPTH) ** -0.25

kernel_name = "hybrid_dilated_moba_gated_deepnorm"


def _layernorm(x, g, b):
    xf = x.astype(jnp.float32)
    mu = xf.mean(-1, keepdims=True)
    var = jnp.square(xf - mu).mean(-1, keepdims=True)
    return ((xf - mu) * lax.rsqrt(var + LN_EPS) * g + b).astype(x.dtype)


def _rope(t, pos):
    half = HEAD_DIM // 2
    inv = ROPE_THETA ** (-jnp.arange(half, dtype=jnp.float32) / half)
    ang = pos.astype(jnp.float32)[:, None] * inv[None, :]
    cos, sin = jnp.cos(ang), jnp.sin(ang)
    t1, t2 = t[..., :half], t[..., half:]
    return jnp.concatenate([t1 * cos - t2 * sin, t1 * sin + t2 * cos], axis=-1).astype(t.dtype)


def _split_heads(t, n_heads):
    B, S, _ = t.shape
    return t.reshape(B, S, n_heads, HEAD_DIM).transpose(0, 2, 1, 3)


def _merge_heads(t):
    B, H, S, D = t.shape
    return t.transpose(0, 2, 1, 3).reshape(B, S, H * D)


def _dilated_window_attention(q, k, v, window, dilation):
    B, H, S, D = q.shape
    d = dilation
    w = window // d
    L = -(-S // d)
    Sp = L * d
    nb = -(-L // w)
    Lp = nb * w

    def to_classes(t):
        t = jnp.pad(t, ((0, 0), (0, 0), (0, Sp - S), (0, 0)))
        return t.reshape(B, H, L, d, D).transpose(0, 1, 3, 2, 4)

    qc, kc, vc = to_classes(q), to_classes(k), to_classes(v)
    qb = jnp.pad(qc, ((0, 0), (0, 0), (0, 0), (0, Lp - L), (0, 0))).reshape(B, H, d, nb, w, D)

    def key_bands(t):
        t = jnp.pad(t, ((0, 0), (0, 0), (0, 0), (w, Lp - L), (0, 0))).reshape(B, H, d, nb + 1, w, D)
        return jnp.concatenate([t[:, :, :, :-1], t[:, :, :, 1:]], axis=4)

    kb, vb = key_bands(kc), key_bands(vc)
    s = jnp.einsum('bhrnqd,bhrnkd->bhrnqk', qb, kb,
                   preferred_element_type=jnp.float32) * (HEAD_DIM ** -0.5)
    i = jnp.arange(w)[None, :, None]
    j = jnp.arange(2 * w)[None, None, :]
    n = jnp.arange(nb)[:, None, None]
    dist = i + w - j
    key_idx = n * w + j - w
    valid = (dist >= 0) & (dist <= w) & (key_idx >= 0)
    s = jnp.where(valid, s, NEG_INF)
    m = s.max(-1, keepdims=True)
    p = jnp.exp(s - m)
    l = p.sum(-1, keepdims=True)
    o = jnp.einsum('bhrnqk,bhrnkd->bhrnqd', (p / l).astype(v.dtype), vb)
    lse = (m + jnp.log(l))[..., 0]
    o = o.reshape(B, H, d, Lp, D)[:, :, :, :L].transpose(0, 1, 3, 2, 4).reshape(B, H, Sp, D)[:, :, :S]
    lse = lse.reshape(B, H, d, Lp)[..., :L].transpose(0, 1, 3, 2).reshape(B, H, Sp)[:, :, :S]
    return o, lse


def _dilated_mixture(q, k, v):
    outs, lses = [], []
    for window, dilation in DILATED_PATTERNS:
        o, lse = _dilated_window_attention(q, k, v, window, dilation)
        outs.append(o)
        lses.append(lse)
    wts = jax.nn.softmax(jnp.stack(lses, axis=0), axis=0)
    o = jnp.stack(outs, axis=0).astype(jnp.float32)
    return jnp.sum(wts[..., None] * o, axis=0).astype(q.dtype)


def _moba_attention(q, k, v):
    B, H, S, D = q.shape
    BS, QC = MOBA_BLOCK, MOBA_Q_CHUNK
    nb = -(-S // BS)
    Sp = nb * BS
    nq = Sp // QC
    topk = min(MOBA_TOPK, nb)
    pad = ((0, 0), (0, 0), (0, Sp - S), (0, 0))
    q, k, v = jnp.pad(q, pad), jnp.pad(k, pad), jnp.pad(v, pad)
    kb = k.reshape(B, H, nb, BS, D)
    vb = v.reshape(B, H, nb, BS, D)
    k_mean = kb.astype(jnp.float32).mean(axis=3)
    gate = jnp.einsum('bhsd,bhnd->bhsn', q.astype(jnp.float32), k_mean)
    own = jnp.arange(Sp) // BS
    past = jnp.arange(nb)[None, :] < own[:, None]
    gate = jnp.where(past, gate, NEG_INF)
    _, sel = lax.top_k(gate, topk)

    q_c = q.reshape(B, H, nq, QC, D).transpose(0, 2, 1, 3, 4).reshape(B * nq, H, QC, D)
    sel_c = sel.reshape(B, H, nq, QC, topk).transpose(0, 2, 1, 3, 4).reshape(B * nq, H, QC, topk)
    b_ids = jnp.repeat(jnp.arange(B, dtype=jnp.int32), nq)
    c_ids = jnp.tile(jnp.arange(nq, dtype=jnp.int32), B)
    h_ids = jnp.arange(H)[:, None, None]
    scale = HEAD_DIM ** -0.5

    def chunk(args):
        b_idx, c_idx, q_i, sel_i = args
        kb_b, vb_b = kb[b_idx], vb[b_idx]
        own_blk = (c_idx * QC) // BS
        k_sel = kb_b[h_ids, sel_i]
        v_sel = vb_b[h_ids, sel_i]
        k_own = lax.dynamic_index_in_dim(kb_b, own_blk, axis=1, keepdims=False)
        v_own = lax.dynamic_index_in_dim(vb_b, own_blk, axis=1, keepdims=False)
        s_sel = jnp.einsum('hqd,hqtkd->hqtk', q_i, k_sel, preferred_element_type=jnp.float32) * scale
        s_sel = jnp.where((sel_i < own_blk)[..., None], s_sel, NEG_INF).reshape(H, QC, topk * BS)
        s_own = jnp.einsum('hqd,hkd->hqk', q_i, k_own, preferred_element_type=jnp.float32) * scale
        qpos = c_idx * QC + jnp.arange(QC)
        kpos = own_blk * BS + jnp.arange(BS)
        s_own = jnp.where(kpos[None, :] <= qpos[:, None], s_own, NEG_INF)
        p = jax.nn.softmax(jnp.concatenate([s_sel, s_own], axis=-1), axis=-1).astype(v.dtype)
        p_sel = p[..., :topk * BS].reshape(H, QC, topk, BS)
        p_own = p[..., topk * BS:]
        return (jnp.einsum('hqtk,hqtkd->hqd', p_sel, v_sel)
                + jnp.einsum('hqk,hkd->hqd', p_own, v_own))

    o = lax.map(chunk, (b_ids, c_ids, q_c, sel_c))
    o = o.reshape(B, nq, H, QC, D).transpose(0, 2, 1, 3, 4).reshape(B, H, Sp, D)
    return o[:, :, :S]


def setup_inputs(seed: int = 0) -> dict:
    key = jax.random.key(seed)
    ks = jax.random.split(key, 8)
    x = jax.random.normal(ks[0], (BATCH, SEQ, D_MODEL), jnp.float32)
    c = jax.random.normal(ks[1], (BATCH, D_MODEL), jnp.float32)
    col_scale = jnp.concatenate([
        jnp.ones((3 * D_A,), jnp.float32).at[2 * D_A:].set(DEEPNORM_BETA),
        jnp.ones((D_A,), jnp.float32),
        jnp.ones((3 * D_B,), jnp.float32).at[2 * D_B:].set(DEEPNORM_BETA),
        jnp.ones((D_B,), jnp.float32)])
    w_in = jax.random.normal(ks[2], (DEPTH, D_MODEL, 4 * D_MIX), jnp.float32) * (D_MODEL ** -0.5) * col_scale
    w_out = jax.random.normal(ks[3], (DEPTH, D_MIX, D_MODEL), jnp.float32) * (D_MIX ** -0.5) * DEEPNORM_BETA
    w_ada = jax.random.normal(ks[4], (DEPTH, D_MODEL, 3 * D_MODEL), jnp.float32) * (0.5 * D_MODEL ** -0.5)
    b_ada = 0.02 * jax.random.normal(ks[5], (DEPTH, 3 * D_MODEL), jnp.float32)
    ln_g = 1.0 + 0.02 * jax.random.normal(ks[6], (DEPTH, D_MODEL), jnp.float32)
    ln_b = 0.02 * jax.random.normal(ks[7], (DEPTH, D_MODEL), jnp.float32)
    return {"x": x, "c": c, "w_in": w_in, "w_out": w_out, "w_ada": w_ada,
            "b_ada": b_ada, "ln_g": ln_g, "ln_b": ln_b}


def reference(x, c, w_in, w_out, w_ada, b_ada, ln_g, ln_b):
    B, S, _ = x.shape
    pos = jnp.arange(S, dtype=jnp.int32)
    offs = np.cumsum([0, D_A, D_A, D_A, D_A, D_B, D_B, D_B])[1:].tolist()
    for layer in range(DEPTH):
        mod = c @ w_ada[layer] + b_ada[layer]
        shift, scale, gate = jnp.split(mod, 3, axis=-1)
        h = x * (1.0 + scale[:, None, :]) + shift[:, None, :]
        proj = h @ w_in[layer]
        qa, ka, va, ga, qb, kb, vb, gb = jnp.split(proj, offs, axis=-1)
        qa, ka, va = _split_heads(qa, N_HEADS_A), _split_heads(ka, N_HEADS_A), _split_heads(va, N_HEADS_A)
        qb, kb, vb = _split_heads(qb, N_HEADS_B), _split_heads(kb, N_HEADS_B), _split_heads(vb, N_HEADS_B)
        qa, ka, qb, kb = _rope(qa, pos), _rope(ka, pos), _rope(qb, pos), _rope(kb, pos)
        o_a = _merge_heads(_dilated_mixture(qa, ka, va)) * jax.nn.silu(ga)
        o_b = _merge_heads(_moba_attention(qb, kb, vb)) * jax.nn.silu(gb)
        y = jnp.concatenate([o_a, o_b], axis=-1) @ w_out[layer]
        x = _layernorm(DEEPNORM_ALPHA * x + gate[:, None, :] * y, ln_g[layer], ln_b[layer])
    return x
```

```python
import numpy as np
import ml_dtypes
from contextlib import ExitStack
import concourse.bass as bass
import concourse.mybir as mybir
from concourse.bass_utils import run_bass_kernel_spmd

F32 = mybir.dt.float32
BF16 = mybir.dt.bfloat16
AF = mybir.ActivationFunctionType
ALU = mybir.AluOpType
AX = mybir.AxisListType

NCORES = 8
BPC = 2
S = 2048
D = 1024
BIG = 480.0
ALPHA = 2.0 ** 0.25
LN_EPS = 1e-5
MASKW = 2816
N_WARM = 0


class Sched:
    def __init__(self, nc, stack):
        self.nc = nc
        self.stack = stack
        self.streams = {k: [] for k in ("pe", "act", "dve", "pool", "sp")}
        self.sems = {}
        self.cnt = {}
        self.waited = {k: {} for k in self.streams}

    def _sem(self, name):
        if name not in self.sems:
            self.sems[name] = self.stack.enter_context(self.nc.semaphore(name))
            self.cnt[name] = 0
        return self.sems[name]

    def op(self, eng, fn, waits=(), sig=None, dma=False):
        for w in waits:
            if w is None:
                continue
            name, val = w
            if self.waited[eng].get(name, 0) >= val:
                continue
            self.waited[eng][name] = val
            self.streams[eng].append(("w", name, val))
        ev = None
        if sig is True:
            sig = eng
        if sig is not None:
            self._sem(sig)
            self.cnt[sig] += 16 if dma else 1
            ev = (sig, self.cnt[sig])
        self.streams[eng].append(("o", fn, sig, 16 if dma else 1))
        return ev

    def last(self, *names):
        return [(n, self.cnt[n]) for n in names if n in self.cnt and self.cnt[n] > 0]

    def replay(self, name, e):
        for item in self.streams[name]:
            if item[0] == "w":
                e.wait_ge(self.sems[item[1]], item[2])
            else:
                ins = item[1](e)
                if item[2] is not None:
                    ins.then_inc(self.sems[item[2]], item[3])


def build_program():
    nc = bass.Bass("TRN2", target_bir_lowering=False)

    def din(name, shape, dt=F32):
        return nc.dram_tensor(name, list(shape), dt, kind="ExternalInput").ap()

    xT_d = din("xT", [BPC, D, S])
    x_d = din("x", [BPC, S, D])
    cT_d = din("cT", [128, 8, BPC])
    win_d = din("w_in", [D, 4096])
    wout_d = din("w_out", [D, D])
    wada_d = din("w_ada", [D, 3072])
    bada_d = din("bada2", [2, 3072])
    lng_d = din("lng", [128, D])
    lnb_d = din("lnb", [128, D])
    cos_d = din("cosT", [128, S])
    sin_d = din("sinT", [128, S])
    maskA_d = din("maskA", [128, MASKW], BF16)
    tri_d = din("tri", [128, 128], BF16)
    perm_d = din("perm", [128, 128], BF16)
    identf_d = din("identf", [128, 128])
    identb_d = din("identb", [128, 128], BF16)
    e8_d = din("E8", [8, S], BF16)
    gconst_d = din("gconst", [128, 3, 128])
    sel2_d = din("sel2", [2, 2, 128])
    y_d = nc.dram_tensor("y", [BPC, S, D], F32, kind="ExternalOutput").ap()

    with ExitStack() as st:
        def sb(name, shape, dt):
            return st.enter_context(nc.sbuf_tensor(name, list(shape), dt))

        def pst(name, shape, dt=F32):
            return st.enter_context(nc.psum_tensor(name, list(shape), dt))

        cosT = sb("cosT_sb", [128, S], F32)
        sinT = sb("sinT_sb", [128, S], F32)
        maskA = sb("maskA_sb", [128, MASKW], BF16)
        tri = sb("tri_sb", [128, 128], BF16)
        perm = sb("perm_sb", [128, 128], BF16)
        identf = sb("identf_sb", [128, 128], F32)
        identb = sb("identb_sb", [128, 128], BF16)
        gconst = sb("gconst_sb", [128, 3, 128], F32)
        sel2 = sb("sel2_sb", [2, 2, 128], F32)
        cT = sb("cT_sb", [128, 8, BPC], F32)
        lng = sb("lng_sb", [128, D], F32)
        lnb = sb("lnb_sb", [128, D], F32)
        gate_bc = sb("gate_bc", [128, BPC, D], F32)
        sc = sb("sc_sb", [128, 16, BPC], F32)
        modp = [sb(f"modp{i}", [2, 512], F32) for i in range(2)]
        hT = sb("hT", [128, 8, S], BF16)
        catT = sb("catT", [128, 8, S], BF16)
        xs = [sb(f"xs{i}", [128, 512], F32) for i in range(2)]
        wball = sb("wball", [128, 8192], BF16)
        QT = [[sb(f"QT{i}{h}", [128, S], BF16) for h in range(2)] for i in range(2)]
        KT = [[sb(f"KT{i}{h}", [128, S], BF16) for h in range(2)] for i in range(2)]
        Vt = [sb(f"Vt{i}", [128, 16, 2, 128], BF16) for i in range(2)]
        pbuf = [sb(f"pbuf{i}", [128, 2, 512], BF16) for i in range(3)]
        qb = [sb(f"qb{i}", [128, 512], BF16) for i in range(2)]
        tmpf = sb("tmpf", [128, 2048], F32)
        rcp = [sb(f"rcp{i}", [128, 512], F32) for i in range(2)]
        kms2 = sb("kms2", [128, 2, 8], F32)
        kmhl2 = sb("kmhl2", [128, 2, 16], BF16)
        g16_2 = sb("g16_2", [128, 2, 256], F32)
        gate2 = sb("gate2", [128, 2, 16, 8], F32)
        top8_2 = sb("top8_2", [128, 2, 16, 8], F32)
        selt2 = sb("selt2", [128, 2, 16, 8], F32)
        nmpad2 = sb("nmpad2", [128, 2, 16, 72], BF16)
        epst = sb("epst", [128, 1], F32)
        onet = sb("onet", [128, 1], F32)

        stp = [pst(f"st{i}", [128, 1024]) for i in range(2)]
        otp = [pst(f"ot{i}", [128, 512]) for i in range(2)]
        pp = pst("pp", [128, 512])
        pr = pst("pr", [128, 512])

        wbuf = [wball[:, i * 4096:(i + 1) * 4096].rearrange("p (m s f) -> p m s f", m=8, s=4) for i in range(2)]
        wout = wball[:, :].rearrange("p (m f) -> p m f", m=8)
        t1 = [tmpf[:, 0:512], tmpf[:, 512:1024]]
        t2 = [tmpf[:, 1024:1536], tmpf[:, 1536:2048]]
        rbuf = [tmpf[:, 0:1024], tmpf[:, 1024:2048]]
        sgt = tmpf[:, 0:512]

        S_ = Sched(nc, st)
        op = S_.op

        badaf = wball[0:2, 0:6144].bitcast(F32)
        ev_bada = op("sp", lambda e: e.dma_start(out=badaf, in_=bada_d), sig="ld_bada", dma=True)
        evc = None
        for dst, src in ((cosT, cos_d), (sinT, sin_d), (maskA, maskA_d), (tri, tri_d), (perm, perm_d),
                         (identf, identf_d), (identb, identb_d), (gconst, gconst_d), (sel2, sel2_d),
                         (cT, cT_d), (lng, lng_d), (lnb, lnb_d)):
            evc = op("sp", (lambda d, s: (lambda e: e.dma_start(out=d[:], in_=s)))(dst, src), sig="ld_c", dma=True)
        ev_ms = None
        for i in range(2):
            ev_ms = op("pool", (lambda t: (lambda e: e.memset(t[:], 1.0)))(Vt[i]), sig=True)
        ev_ms = op("pool", lambda e: e.memset(nmpad2[:], 0.0), sig=True)
        ev_ms = op("pool", lambda e: e.memset(epst[:], LN_EPS), sig=True)
        ev_ms = op("pool", lambda e: e.memset(onet[:], 1.0), sig=True)

        xs_free = [None, None]
        modp_free = [None, None]
        ldn = 0
        tr_idx = 0
        ev_pr_free = None
        ev_st_free = [None, None]
        ev_sc = None
        ev_gate = None
        last_tr = None
        wreg = [hT[:, :, :].rearrange("p a b -> p (a b)").bitcast(F32).rearrange("p (m f) -> p m f", m=8),
                catT[:, :, :].rearrange("p a b -> p (a b)").bitcast(F32).rearrange("p (m f) -> p m f", m=8)]
        wreg_free = [None, None]
        wld = {}

        def wada_load(third):
            r = 0 if third == 0 else 1
            src = wada_d[:, third * 1024:(third + 1) * 1024].rearrange("(m p) f -> p m f", p=128)
            wld[third] = op("sp", (lambda d, s_: (lambda e: e.dma_start(out=d, in_=s_)))(wreg[r], src),
                            waits=[wreg_free[r]], sig=f"ld_wa{r}", dma=True)

        wada_load(0)
        wada_load(1)
        Xst = [QT[i][h][:, :].bitcast(F32) for i in range(2) for h in range(2)] + \
              [KT[i][h][:, :].bitcast(F32) for i in range(2) for h in range(2)]
        xst_ld = []
        for idx in range(8):
            mc, half = idx // 2, idx % 2
            xst_ld.append(op("sp", (lambda d, s_: (lambda e: e.dma_start(out=d, in_=s_)))(
                Xst[idx], xT_d[0, mc * 128:(mc + 1) * 128, half * 1024:(half + 1) * 1024]), sig=f"ld_xs{idx}", dma=True))
        for g6 in range(6):
            third, hf = g6 // 2, g6 % 2
            r = 0 if third == 0 else 1
            ev_mm = None
            for mc in range(8):
                last = (mc == 7)
                ev_mm = op("pe", (lambda l, r_, s0, s1: (lambda e: e.matmul(pp[0:2, :], l, r_, start=s0, stop=s1)))(
                    cT[:, mc, :], wreg[r][:, mc, hf * 512:(hf + 1) * 512], mc == 0, last),
                    waits=[wld[third], evc, S_.last("dve")[0] if (mc == 0 and S_.last("dve")) else None],
                    sig=(True if last else None))
            if hf == 1:
                wreg_free[r] = ev_mm
                if third == 1:
                    wada_load(2)
            mi = g6 % 2
            addc = 1.0 if g6 in (2, 3) else 0.0
            ev_mod = op("dve", (lambda m, a, g: (lambda e: e.scalar_tensor_tensor(
                out=m[0:2, :], in0=pp[0:2, :], scalar=a, in1=badaf[0:2, g * 512:(g + 1) * 512],
                op0=ALU.add, op1=ALU.add)))(modp[mi], addc, g6),
                waits=[ev_mm, modp_free[mi], evc, ev_bada], sig=True)
            if g6 < 4:
                for c4 in range(4):
                    idx = g6 * 4 + c4
                    last_tr = op("pe", (lambda m, c, ix: (lambda e: e.matmul(
                        pr[:, ix * 2:(ix + 1) * 2], m[0:2, c * 128:(c + 1) * 128], identf[0:2, 0:2],
                        start=True, stop=True)))(modp[mi], c4, idx),
                        waits=[ev_mod], sig=True)
                modp_free[mi] = last_tr
                if g6 == 3:
                    ev_sc = op("dve", lambda e: e.tensor_copy(
                        sc[:, :, :], pr[:, 0:32].rearrange("p (i b) -> p i b", b=2)),
                        waits=[last_tr], sig=True)
                    ev_pr_free = ev_sc
            else:
                for b in range(BPC):
                    ev_g = op("pe", (lambda m, bb: (lambda e: e.matmul(
                        stp[bb][:, 0:512], sel2[0:2, bb, :], m[0:2, :], start=True, stop=True)))(modp[mi], b),
                        waits=[ev_mod, ev_st_free[b]], sig=True)
                    ev_gate = op("dve", (lambda bb, g: (lambda e: e.tensor_copy(
                        gate_bc[:, bb, (g - 4) * 512:(g - 3) * 512], stp[bb][:, 0:512])))(b, g6),
                        waits=[ev_g], sig=True)
                    ev_st_free[b] = ev_gate
                    modp_free[mi] = ev_g

        state = {
            "pp_free": [ev_sc], "pr_free": [ev_sc],
            "qb_free": [None, None], "t_free": [None, None],
            "st_free": [[ev_st_free[0]], [ev_st_free[1]]],
            "ot_free": [[], []], "pb_free": [[], [], []],
            "wb_free": [[], []], "wb_ld": [None, None],
            "qk_free": [[], []], "vt_free": [[], []],
            "nm_free": [],
        }
        setup_done = [ev_sc, ev_gate, ev_ms, evc]

        def unit_cols(u):
            base = 0 if u < 4 else 2048
            uu = u % 4
            return [base + s * 512 + uu * 128 for s in range(4)]

        def load_weights(b, u, extra_waits=()):
            wb = u % 2
            cols = unit_cols(u)
            ev = None
            for s in range(4):
                src = win_d[:, cols[s]:cols[s] + 128].rearrange("(m p) f -> p m f", p=128)
                ev = op("pool", (lambda d, s_: (lambda e: e.dma_start(out=d, in_=s_)))(wbuf[wb][:, :, s, :], src),
                        waits=list(state["wb_free"][wb]) + list(extra_waits), sig=f"ld_w{wb}", dma=True)
            state["wb_ld"][wb] = ev
            state["wb_free"][wb] = []

        def proj_unit(b, u, hT_ready, full=True, silu_u=None):
            wb = u % 2
            qi = u % 2
            is_b = u >= 4
            first_waits = list(state["qk_free"][qi]) + list(state["vt_free"][qi])
            state["qk_free"][qi] = []
            state["vt_free"][qi] = []
            extra_ready = []
            if u in (4, 5):
                extra_ready.append(op("sp", lambda e: e.dma_start(out=KT[qi][0][64:72, :], in_=e8_d),
                                      waits=first_waits, sig=f"ld_e8_u{u}", dma=True))
            Pb = [[stp[0][:, 0:512], list(state["st_free"][0])], [stp[0][:, 512:1024], list(state["st_free"][0])],
                  [stp[1][:, 0:512], list(state["st_free"][1])], [stp[1][:, 512:1024], list(state["st_free"][1])]]
            Rb = [[pr[:, :], list(state["pr_free"])], [pp[:, :], list(state["pp_free"])],
                  [otp[0][:, :], list(state["ot_free"][0])], [otp[1][:, :], list(state["ot_free"][1])]]
            cnt = {"p": 0, "r": 0, "k": 0, "g": 0}
            last_pe = [None]

            def mm_group(bank, s, tok):
                ev = None
                for mc in range(8):
                    ev = op("pe", (lambda o, l, r, s0, s1: (lambda e: e.matmul(o, l, r, start=s0, stop=s1)))(
                        bank[0], wbuf[wb][:, mc, s, :], hT[:, mc, tok], mc == 0, mc == 7),
                        waits=([state["wb_ld"][wb]] + list(hT_ready) + list(bank[1]) if mc == 0 else ()),
                        sig=(True if mc == 7 else None))
                last_pe[0] = ev
                return ev

            def qk_step(s, tt):
                tok = slice(tt * 512, (tt + 1) * 512)
                bank = Pb[cnt["p"] % 4]
                cnt["p"] += 1
                rbank = Rb[cnt["r"] % 4]
                cnt["r"] += 1
                k = cnt["k"] % 2
                cnt["k"] += 1
                dst = (QT if s == 0 else KT)[qi]
                ev = mm_group(bank, s, tok)
                ev_a = op("act", lambda e: e.activation(qb[k][:, :], bank[0], AF.Copy),
                          waits=[ev, state["qb_free"][k]], sig=True)
                ev_t1 = op("dve", lambda e: e.tensor_tensor(t1[k], bank[0], cosT[:, tok], op=ALU.mult),
                           waits=[ev, ev_a, state["t_free"][k]], sig=True)
                bank[1] = [ev_a, ev_t1]

                def later():
                    ev_p = op("pe", lambda e: e.matmul(rbank[0], perm[:, :], qb[k][:, :], start=True, stop=True),
                              waits=[ev_a] + list(rbank[1]), sig=True)
                    state["qb_free"][k] = ev_p
                    ev_t2 = op("dve", lambda e: e.tensor_tensor(t2[k], rbank[0], sinT[:, tok], op=ALU.mult),
                               waits=[ev_p, ev_t1], sig=True)
                    rbank[1] = [ev_t2]
                    if not is_b:
                        ev_lo = op("dve", lambda e: e.tensor_tensor(dst[0][:, tok], t1[k], t2[k], op=ALU.add),
                                   waits=[ev_t2] + first_waits, sig=True)
                    else:
                        ev_lo = op("pool", lambda e: e.tensor_tensor(dst[0][0:64, tok], t1[k][0:64, :], t2[k][0:64, :], op=ALU.add),
                                   waits=[ev_t2] + first_waits, sig=True)
                        op("dve", lambda e: e.tensor_tensor(dst[1][0:64, tok], t1[k][64:128, :], t2[k][64:128, :], op=ALU.add),
                           waits=[ev_t2] + first_waits, sig=True)
                    state["t_free"][k] = ev_lo
                return later

            def g_step(tt):
                tok = slice(tt * 512, (tt + 1) * 512)
                bank = Pb[cnt["p"] % 4]
                cnt["p"] += 1
                gi = cnt["g"] % 2
                cnt["g"] += 1
                sg = rcp[gi][:, :]
                ev = mm_group(bank, 3, tok)

                def later():
                    ea = op("act", lambda e: e.activation(sg, bank[0], AF.Exp, scale=-1.0),
                            waits=[ev] + list(state["rc_free"][gi]), sig=True)
                    ea = op("act", lambda e: e.activation(sg, sg, AF.Ln, bias=onet[:, 0:1]), waits=[ea], sig=True)
                    ea = op("act", lambda e: e.activation(sg, sg, AF.Exp, scale=-1.0), waits=[ea], sig=True)
                    ev_d = op("dve", lambda e: e.tensor_tensor(catT[:, u, tok], bank[0], sg, op=ALU.mult),
                              waits=[ea, ev] + first_waits, sig=True)
                    bank[1] = [ev_d]
                    state["rc_free"][gi] = [ev_d]
                return later

            def v_step(kg):
                rbank = Rb[cnt["r"] % 4]
                cnt["r"] += 1
                ev = None
                for i4 in range(4):
                    kb = kg * 4 + i4
                    for mc in range(8):
                        ev = op("pe", (lambda o, l, r, s0, s1: (lambda e: e.matmul(o, l, r, start=s0, stop=s1)))(
                            rbank[0][:, i4 * 128:(i4 + 1) * 128], hT[:, mc, kb * 128:(kb + 1) * 128], wbuf[wb][:, mc, 2, :],
                            mc == 0, mc == 7),
                            waits=([state["wb_ld"][wb]] + list(hT_ready) + list(rbank[1]) if (mc == 0 and i4 == 0) else ()),
                            sig=(True if (mc == 7 and i4 == 3) else None))
                last_pe[0] = ev

                def later():
                    prv = rbank[0].rearrange("p (i h d) -> p i h d", i=4, h=2)
                    op("act", lambda e: e.activation(Vt[qi][:, kg * 4:(kg + 1) * 4, 0, 0:64], prv[:, :, 0, :], AF.Copy),
                       waits=[ev] + first_waits, sig=True)
                    ev1 = op("act", lambda e: e.activation(Vt[qi][:, kg * 4:(kg + 1) * 4, 1, 64:128], prv[:, :, 1, :], AF.Copy),
                             waits=[ev] + first_waits, sig=True)
                    rbank[1] = [ev1]
                return later

            kmean_ev = [None]

            def kmean_step():
                kts = KT[qi]
                kdone = S_.last("dve", "pool")
                e1 = [op("dve", (lambda h: (lambda e: e.tensor_reduce(
                    out=kms2[0:64, h, :], in_=kts[h][0:64, :].rearrange("p (n k) -> p n k", k=256), axis=AX.X, op=ALU.add)))(hh),
                    waits=kdone + list(state["nm_free"]), sig=True) for hh in range(2)]
                e2 = op("dve", lambda e: e.tensor_scalar(kmhl2[0:64, :, 0:8], kms2[0:64, :, :], 1.0 / 256.0, None, op0=ALU.mult),
                        waits=e1, sig=True)
                kmean_ev[0] = op("dve", lambda e: e.scalar_tensor_tensor(
                    out=kmhl2[0:64, :, 8:16], in0=kms2[0:64, :, :], scalar=1.0 / 256.0, in1=kmhl2[0:64, :, 0:8],
                    op0=ALU.mult, op1=ALU.subtract), waits=[e2], sig=True)
                return None

            def flush_step():
                return None

            def silu_step(tt):
                tok = slice(tt * 512, (tt + 1) * 512)
                gi = cnt["g"] % 2
                cnt["g"] += 1
                sg = rcp[gi][:, :]
                su = silu_u
                ea = op("act", lambda e: e.activation(sg, catT[:, su, tok], AF.Exp, scale=-1.0),
                        waits=list(state["rc_free"][gi]) + list(state["fill_done"]), sig=True)
                ea = op("act", lambda e: e.activation(sg, sg, AF.Ln, bias=onet[:, 0:1]), waits=[ea], sig=True)
                ea = op("act", lambda e: e.activation(sg, sg, AF.Exp, scale=-1.0), waits=[ea], sig=True)
                ev_d = op("dve", lambda e: e.tensor_tensor(catT[:, su, tok], catT[:, su, tok], sg, op=ALU.mult),
                          waits=[ea], sig=True)
                state["rc_free"][gi] = [ev_d]
                return None

            if full:
                steps = [lambda tt=tt: qk_step(1, tt) for tt in range(4)] + [lambda kg=kg: v_step(kg) for kg in range(4)] \
                    + [lambda tt=tt: qk_step(0, tt) for tt in range(4)] + [lambda tt=tt: g_step(tt) for tt in range(4)]
            else:
                steps = []
                for tt in range(4):
                    steps += [lambda tt=tt: qk_step(1, tt)]
                    if silu_u is not None:
                        steps += [lambda tt=tt: silu_step(tt)]
                if silu_u is None:
                    steps += [flush_step]
                nk = len(steps)
                steps += [lambda tt=tt: qk_step(0, tt) for tt in range(4)]
            pending = None
            for stf in steps:
                fin = stf()
                if pending is not None:
                    pending()
                pending = fin
                yield
            if pending is not None:
                pending()
            state["wb_free"][wb] = [last_pe[0]]
            state["pr_free"], state["pp_free"] = list(Rb[0][1]), list(Rb[1][1])
            state["ot_free"] = [list(Rb[2][1]), list(Rb[3][1])]
            state["st_free"] = [list(Pb[0][1]) + list(Pb[1][1]), list(Pb[2][1]) + list(Pb[3][1])]
            state["qk_done"] = S_.last("dve", "pool", "act") + extra_ready
            state["kmean_ev"] = kmean_ev[0]

        def silu_only(b, u):
            for tt in range(4):
                tok = slice(tt * 512, (tt + 1) * 512)
                gi = tt % 2
                sg = rcp[gi][:, :]
                ea = op("act", (lambda sg_, tk: (lambda e: e.activation(sg_, catT[:, u, tk], AF.Exp, scale=-1.0)))(sg, tok),
                        waits=list(state["rc_free"][gi]) + list(state["fill_done"]), sig=True)
                ea = op("act", (lambda sg_: (lambda e: e.activation(sg_, sg_, AF.Ln, bias=onet[:, 0:1])))(sg), waits=[ea], sig=True)
                ea = op("act", (lambda sg_: (lambda e: e.activation(sg_, sg_, AF.Exp, scale=-1.0)))(sg), waits=[ea], sig=True)
                ev_d = op("dve", (lambda sg_, tk: (lambda e: e.tensor_tensor(catT[:, u, tk], catT[:, u, tk], sg_, op=ALU.mult)))(sg, tok),
                          waits=[ea], sig=True)
                state["rc_free"][gi] = [ev_d]

        def gating_gen(b, u):
            qi = u % 2
            rope_done = list(state["qk_done"])
            qts, kts = QT[qi], KT[qi]
            gc = gconst[:, :, :].rearrange("p t (a n) -> p t a n", n=8)
            e1 = []
            for hh in range(2):
                e1.append(op("dve", (lambda h: (lambda e: e.tensor_reduce(
                    out=kms2[0:64, h, :], in_=kts[h][0:64, :].rearrange("p (n k) -> p n k", k=256), axis=AX.X, op=ALU.add)))(hh),
                    waits=rope_done + list(state["nm_free"]), sig=True))
                yield
            e2 = op("dve", lambda e: e.tensor_scalar(kmhl2[0:64, :, 0:8], kms2[0:64, :, :], 1.0 / 256.0, None, op0=ALU.mult),
                    waits=e1, sig=True)
            yield
            e3 = op("dve", lambda e: e.scalar_tensor_tensor(
                out=kmhl2[0:64, :, 8:16], in0=kms2[0:64, :, :], scalar=1.0 / 256.0, in1=kmhl2[0:64, :, 0:8],
                op0=ALU.mult, op1=ALU.subtract), waits=[e2], sig=True)
            yield
            evg = []
            for hh in range(2):
                ev = None
                for q16 in range(16):
                    ev = op("pe", (lambda q, h: (lambda e: e.matmul(
                        pp[:, h * 256 + q * 16:h * 256 + (q + 1) * 16], qts[h][0:64, q * 128:(q + 1) * 128], kmhl2[0:64, h, :],
                        start=True, stop=True)))(q16, hh),
                        waits=([e3] + rope_done + list(state["pp_free"]) if (q16 == 0 and hh == 0) else ()),
                        sig=(True if q16 == 15 else None))
                evg.append(ev)
                yield
            yield
            e4 = [op("dve", lambda e: e.tensor_copy(g16_2[:, :, :], pp[:, :].rearrange("p (h c) -> p h c", h=2)),
                     waits=[evg[1]], sig=True)]
            state["pp_free"] = [e4[0]]
            yield
            g16v = g16_2[:, :, :].rearrange("p h (a c) -> p h a c", c=16)
            e5 = op("dve", lambda e: e.tensor_tensor(gate2[:, :, :, :], g16v[:, :, :, 0:8], g16v[:, :, :, 8:16], op=ALU.add),
                    waits=e4, sig=True)
            yield
            e6 = None
            for hh in range(2):
                e6 = op("dve", (lambda h: (lambda e: e.tensor_tensor(gate2[:, h, :, :], gate2[:, h, :, :], gc[:, 0, :, :], op=ALU.add)))(hh),
                        waits=[e5], sig=True)
            yield
            e7 = None
            for hh in range(2):
                for q16 in range(16):
                    e7 = op("dve", (lambda q, h: (lambda e: e.max(top8_2[:, h, q, :], gate2[:, h, q, :])))(q16, hh),
                            waits=[e6], sig=(True if (q16 == 15 and hh == 1) else None))
                yield
            e8 = op("dve", lambda e: e.tensor_tensor(selt2[:, :, :, :], gate2[:, :, :, :],
                                                     top8_2[:, :, :, 2:3].to_broadcast([128, 2, 16, 8]), op=ALU.is_ge),
                    waits=[e7], sig=True)
            yield
            e9 = None
            for hh in range(2):
                e9 = op("dve", (lambda h: (lambda e: e.tensor_tensor(selt2[:, h, :, :], selt2[:, h, :, :], gc[:, 1, :, :], op=ALU.mult)))(hh),
                        waits=[e8], sig=True)
            yield
            e10 = None
            for hh in range(2):
                e10 = op("dve", (lambda h: (lambda e: e.tensor_tensor(selt2[:, h, :, :], selt2[:, h, :, :], gc[:, 2, :, :], op=ALU.add)))(hh),
                         waits=[e9], sig=True)
            yield
            e11 = op("dve", lambda e: e.tensor_scalar(nmpad2[:, :, :, 64:72], selt2[:, :, :, :], BIG, -BIG,
                                                      op0=ALU.mult, op1=ALU.add),
                     waits=[e10] + list(state["nm_free"]), sig=True)
            yield
            n = 0
            ev = None
            for hh in range(2):
                for piece in range(4):
                    bank = pp
                    fkey = "pp_free"
                    n += 1
                    for i4 in range(4):
                        q16 = piece * 4 + i4
                        ev = op("pe", (lambda q, i_, h, bk: (lambda e: e.matmul(
                            bk[0:72, i_ * 128:(i_ + 1) * 128], nmpad2[:, h, q, 0:72], identb[:, :],
                            start=True, stop=True)))(q16, i4, hh, bank),
                            waits=([e11] + list(state[fkey]) if i4 == 0 else ()),
                            sig=(True if i4 == 3 else None))
                    yield
                    e12 = op("dve", (lambda pc, h, bk: (lambda e: e.tensor_copy(
                        qts[h][64:72, pc * 512:(pc + 1) * 512], bk[64:72, :])))(piece, hh, bank),
                        waits=[ev], sig=True)
                    state[fkey] = [e12]
                    yield
            state["nm_free"] = [ev]

        def vg_filler(b, u, single=False):
            wb = u % 2
            qi = u % 2
            fw = list(state["vt_free"][qi])
            banks = [[pr[:, :], "pr_free"]] if single else [[pr[:, :], "pr_free"], [pp[:, :], "pp_free"]]
            n = 0
            hT_ready = state["hT_ready"]
            for kind, idx in [("v", 0), ("g", 0), ("v", 1), ("g", 1), ("v", 2), ("g", 2), ("v", 3), ("g", 3)]:
                bank, fkey = banks[n % len(banks)]
                n += 1
                ev = None
                if kind == "v":
                    for i4 in range(4):
                        kb = idx * 4 + i4
                        for mc in range(8):
                            ev = op("pe", (lambda o, l, r, s0, s1: (lambda e: e.matmul(o, l, r, start=s0, stop=s1)))(
                                bank[:, i4 * 128:(i4 + 1) * 128], hT[:, mc, kb * 128:(kb + 1) * 128], wbuf[wb][:, mc, 2, :],
                                mc == 0, mc == 7),
                                waits=([state["wb_ld"][wb]] + list(hT_ready) + list(state[fkey]) if (mc == 0 and i4 == 0) else ()),
                                sig=(True if (mc == 7 and i4 == 3) else None))
                        if (not single) or i4 % 2 == 1:
                            yield
                    prv = bank.rearrange("p (i h d) -> p i h d", i=4, h=2)
                    op("dve", (lambda pv, kg: (lambda e: e.tensor_copy(Vt[qi][:, kg * 4:(kg + 1) * 4, 0, 0:64], pv[:, :, 0, :])))(prv, idx),
                       waits=[ev] + fw, sig=True)
                    ev1 = op("dve", (lambda pv, kg: (lambda e: e.tensor_copy(Vt[qi][:, kg * 4:(kg + 1) * 4, 1, 64:128], pv[:, :, 1, :])))(prv, idx),
                             waits=[ev] + fw, sig=True)
                    state[fkey] = [ev1]
                else:
                    tok = slice(idx * 512, (idx + 1) * 512)
                    for mc in range(8):
                        ev = op("pe", (lambda o, l, r, s0, s1: (lambda e: e.matmul(o, l, r, start=s0, stop=s1)))(
                            bank, wbuf[wb][:, mc, 3, :], hT[:, mc, tok], mc == 0, mc == 7),
                            waits=([state["wb_ld"][wb]] + list(hT_ready) + list(state[fkey]) if mc == 0 else ()),
                            sig=(True if mc == 7 else None))
                        if (mc % 2 == 1 and not single) or (mc % 4 == 3 and single):
                            yield
                    ev1 = op("dve", (lambda bk, tk: (lambda e: e.tensor_copy(catT[:, u, tk], bk)))(bank, tok),
                             waits=[ev], sig=True)
                    state[fkey] = [ev1]
                state["fill_done"] = [ev1]
                state["fill_last_pe"] = [ev]
                yield
                if single:
                    yield

        def att_unit(b, u, ready):
            qi = u % 2
            is_b = u >= 4
            K = 72 if is_b else 64
            groups = []
            for hh in range(2):
                for qt in range(4):
                    ng = 2 * qt + 2
                    for j in range(ng):
                        groups.append((hh, qt, j, j == ng - 1))
            n = len(groups)
            exp_ev = [None] * n
            msk_ev = [None] * n
            ot_of = {}
            last_norm = []
            for i in range(n + 2):
                if i < n:
                    hh, qt, j, lastg = groups[i]
                    sti = state["gcount"] % 2
                    pbi = state["gcount"] % 3
                    state["gcount"] += 1
                    if is_b:
                        qsl, ksl, rows = QT[qi][hh], KT[qi][hh], slice(0, K)
                    else:
                        qsl, ksl, rows = QT[qi][0], KT[qi][0], slice(64 * hh, 64 * hh + 64)
                    kbs = (2 * j, 2 * j + 1)
                    c0s = [max(0, (kb - 4 * qt) * 128) for kb in kbs]
                    ev = None
                    for jj in range(2):
                        kb, c0 = kbs[jj], c0s[jj]
                        ev = op("pe", (lambda o, l, r: (lambda e: e.matmul(o, l, r, start=True, stop=True)))(
                            stp[sti][:, jj * 512 + c0:(jj + 1) * 512], ksl[rows, kb * 128:(kb + 1) * 128],
                            qsl[rows, qt * 512 + c0:(qt + 1) * 512]),
                            waits=(list(ready) + list(state["st_free"][sti]) if jj == 0 else ()),
                            sig=(True if jj == 1 else None))
                    stv = stp[sti][:, :].rearrange("p (j c) -> p j c", j=2)
                    if c0s[0] == c0s[1]:
                        ev_e = op("act", (lambda pb_, sv, c: (lambda e: e.activation(
                            pb_[:, :, c:512], sv[:, :, c:512], AF.Exp, scale=0.125)))(pbuf[pbi], stv, c0s[0]),
                            waits=[ev] + list(state["pb_free"][pbi]), sig=True)
                    else:
                        for jj in range(2):
                            ev_e = op("act", (lambda pb_, sv, c, j_: (lambda e: e.activation(
                                pb_[:, j_, c:512], sv[:, j_, c:512], AF.Exp, scale=0.125)))(pbuf[pbi], stv, c0s[jj], jj),
                                waits=[ev] + list(state["pb_free"][pbi]), sig=True)
                    state["st_free"][sti] = [ev_e]
                    exp_ev[i] = (ev_e, pbi, c0s)
                    evs_m = []
                    for jj in range(2):
                        kb, c0 = kbs[jj], c0s[jj]
                        if not is_b:
                            o_ = 4 * qt - kb
                            base = 128 * (o_ + 3)
                            eng = "dve"
                            evs_m.append(op(eng, (lambda pb_, j_, c, bs: (lambda e: e.tensor_tensor(
                                pb_[:, j_, c:512], pb_[:, j_, c:512], maskA[:, bs + c:bs + 512], op=ALU.mult)))(
                                pbuf[pbi], jj, c0, base), waits=[ev_e], sig=True))
                        elif kb >= 4 * qt:
                            evs_m.append(op("dve", (lambda pb_, j_, c: (lambda e: e.tensor_tensor(
                                pb_[:, j_, c:c + 128], pb_[:, j_, c:c + 128], tri[:, :], op=ALU.mult)))(
                                pbuf[pbi], jj, c0), waits=[ev_e], sig=True))
                    msk_ev[i] = evs_m
                if i >= 2:
                    i2 = i - 2
                    hh, qt, j, lastg = groups[i2]
                    ev_e, pbi, c0s = exp_ev[i2]
                    key = (hh, qt)
                    if key not in ot_of:
                        ot_of[key] = state["ot_n"] % 2
                        state["ot_n"] += 1
                    oi = ot_of[key]
                    ev = None
                    for jj in range(2):
                        kb, c0 = 2 * j + jj, c0s[jj]
                        first = (j == 0 and jj == 0)
                        ev = op("pe", (lambda o, l, r, f: (lambda e: e.matmul(o, l, r, start=f, stop=False,
                                                                              skip_group_check=True)))(
                            otp[oi][:, c0:512], Vt[qi][:, kb, hh, :], pbuf[pbi][:, jj, c0:512], first),
                            waits=(list(msk_ev[i2]) + [ev_e] + (list(state["ot_free"][oi]) if first else []) if jj == 0 else ()),
                            sig=(True if jj == 1 else None))
                    state["pb_free"][pbi] = [ev]
                    if lastg:
                        tok = slice(qt * 512, (qt + 1) * 512)
                        ri = state["rc_n"] % 2
                        state["rc_n"] += 1
                        orow = slice(0, 64) if hh == 0 else slice(64, 128)
                        lrow = slice(64, 128) if hh == 0 else slice(0, 64)
                        e1a = op("act", (lambda r_, o_, orow_, lrow_: (lambda e: e.activation(r_[orow_, :], o_[lrow_, :], AF.Ln)))(
                            rcp[ri], otp[oi], orow, lrow), waits=[ev] + list(state["rc_free"][ri]), sig=True)
                        e1 = op("act", (lambda r_, orow_: (lambda e: e.activation(r_[orow_, :], r_[orow_, :], AF.Exp, scale=-1.0)))(
                            rcp[ri], orow), waits=[e1a], sig=True)
                        e2 = op("dve", (lambda r_, o_, orow_: (lambda e: e.tensor_tensor(
                            r_[orow_, :], o_[orow_, :], r_[orow_, :], op=ALU.mult)))(rcp[ri], otp[oi], orow),
                            waits=[e1], sig=True)
                        state["ot_free"][oi] = [e2]
                        e3 = op("pool", (lambda n_, orow_, tk: (lambda e: e.tensor_tensor(
                            catT[orow_, u, tk], n_[orow_, :], catT[orow_, u, tk], op=ALU.mult)))(rcp[ri], orow, tok),
                            waits=[e2] + list(ready), sig=True)
                        state["rc_free"][ri] = [e3]
                        last_norm = [e3]
                yield
            state["qk_free"][qi] = S_.last("pe")
            state["vt_free"][qi] = S_.last("pe")
            state["att_done"] = last_norm

        state["gcount"] = 0
        state["ot_n"] = 0
        state["rc_n"] = 0
        state["rc_free"] = [[], []]

        def run(gen):
            for _ in gen:
                pass

        def merge3(ga, gb, gc_):
            done_b = gb is None
            done_c = gc_ is None
            for _ in ga:
                if not done_b:
                    try:
                        next(gb)
                    except StopIteration:
                        done_b = True
                if not done_c:
                    try:
                        next(gc_)
                    except StopIteration:
                        done_c = True
            if not done_b:
                for _ in gb:
                    pass
            if not done_c:
                for _ in gc_:
                    pass

        def merge(ga, gb, ratio):
            acc = 0.0
            gb_done = gb is None
            for _ in ga:
                acc += ratio
                while acc >= 1.0 and not gb_done:
                    acc -= 1.0
                    try:
                        next(gb)
                    except StopIteration:
                        gb_done = True
            if not gb_done:
                for _ in gb:
                    pass

        ev_out = None
        stats4 = [sb(f"stats4_{i}", [128, 2, 6], F32) for i in range(4)]
        mv4 = [sb(f"mv4_{i}", [128, 2], F32) for i in range(4)]
        rstd4 = [sb(f"rstd4_{i}", [128, 1], F32) for i in range(4)]
        Xb = [QT[i][h][:, :].bitcast(F32) for i in range(2) for h in range(2)]
        Rb4 = [KT[i][h][:, :].bitcast(F32) for i in range(2) for h in range(2)]
        fin_free = [[], [], [], []]
        fin_pe_done = []

        def hT_chunk(bn, idx, waits0):
            mc, q4 = idx // 4, idx % 4
            i = idx % 2
            src = xT_d[bn, mc * 128:(mc + 1) * 128, q4 * 512:(q4 + 1) * 512]
            evl = op("sp", (lambda d, s_: (lambda e: e.dma_start(out=d[:, :], in_=s_)))(xs[i], src),
                     waits=[xs_free[i]] + list(waits0), sig=f"ld_x{i}", dma=True)
            eva = op("act", (lambda d, mc_, h_, b_: (lambda e: e.activation(
                hT[:, mc_, h_ * 512:(h_ + 1) * 512], d[:, :], AF.Identity,
                bias=sc[:, mc_, b_:b_ + 1], scale=sc[:, 8 + mc_, b_:b_ + 1])))(xs[i], mc, q4, bn),
                waits=[evl] + list(waits0), sig=True)
            xs_free[i] = eva
            return eva

        hT_ready = []
        xst_free = [None] * 8
        for b in range(BPC):
            bar = S_.last("pe", "act", "dve", "pool") + (setup_done if b == 0 else [])
            if b == 0:
                for idx in range(16):
                    mc, half = idx // 2, idx % 2
                    sidx = idx % 8
                    if idx >= 8:
                        xst_ld[sidx] = op("sp", (lambda d, s_: (lambda e: e.dma_start(out=d, in_=s_)))(
                            Xst[sidx], xT_d[0, mc * 128:(mc + 1) * 128, half * 1024:(half + 1) * 1024]),
                            waits=[xst_free[sidx]], sig=f"ld_xs{sidx}", dma=True)
                    eva = op("act", (lambda d, mc_, h_: (lambda e: e.activation(
                        hT[:, mc_, h_ * 1024:(h_ + 1) * 1024], d, AF.Identity,
                        bias=sc[:, mc_, 0:1], scale=sc[:, 8 + mc_, 0:1])))(Xst[sidx], mc, half),
                        waits=[xst_ld[sidx], ev_sc], sig=True)
                    xst_free[sidx] = eva
                    hT_ready = [eva]
                for i in range(2):
                    for h in range(2):
                        extra = op("sp", (lambda d: (lambda e: e.dma_start(out=d[64:72, :], in_=e8_d)))(KT[i][h]),
                                   waits=hT_ready, sig="ld_e8", dma=True)
                bar = bar + [extra]
                load_weights(b, 0, extra_waits=bar)
                load_weights(b, 1, extra_waits=bar)
                state["qk_free"] = [list(bar), list(bar)]
            else:
                qf = []
                for i in range(2):
                    fr = list(fin_free[2 * i]) + list(fin_free[2 * i + 1]) + list(fin_pe_done)
                    ex = op("sp", (lambda d: (lambda e: e.dma_start(out=d[64:72, :], in_=e8_d)))(KT[i][1]),
                            waits=fr, sig=f"ld_e8b{i}", dma=True)
                    qf.append(fr + [ex])
                state["qk_free"] = qf
            state["hT_ready"] = hT_ready
            state["fill_done"] = []
            def e8_events():
                return [ev_ for ev_ in state["qk_done"] if ev_[0].startswith("ld_e8")]

            e8_all = []
            run(proj_unit(b, 0, hT_ready, full=True))
            load_weights(b, 2)
            run(proj_unit(b, 1, hT_ready, full=False, silu_u=None))
            for u in range(8):
                ready = S_.last("dve", "pool", "act") + e8_all
                if u < 7:
                    gg = gating_gen(b, u + 1) if u + 1 >= 4 else None
                    merge3(att_unit(b, u, ready), vg_filler(b, u + 1, single=(gg is not None)), gg)
                    state["wb_free"][(u + 1) % 2] = list(state["fill_last_pe"])
                    if u + 3 < 8:
                        load_weights(b, u + 3)
                    if u + 2 < 8:
                        run(proj_unit(b, u + 2, hT_ready, full=False, silu_u=u + 1))
                        e8_all += e8_events()
                    else:
                        evw = None
                        for half in range(2):
                            src = wout_d[:, half * 512:(half + 1) * 512].rearrange("(m p) f -> p m f", p=128)
                            evw = op("pool", (lambda s_, h_: (lambda e: e.dma_start(out=wout[:, :, h_ * 512:(h_ + 1) * 512], in_=s_)))(src, half),
                                     waits=list(state["wb_free"][0]) + list(state["wb_free"][1]), sig="ld_wo", dma=True)
                        silu_only(b, u + 1)
                else:
                    for i_att, _ in enumerate(att_unit(b, u, ready)):
                        if i_att == 22:
                            for hf in range(2):
                                ev_wg = op("dve", (lambda h_, b_: (lambda e: e.tensor_tensor(
                                    wout[:, 4 * h_:4 * h_ + 4, :], wout[:, 4 * h_:4 * h_ + 4, :],
                                    gate_bc[:, b_:b_ + 1, :].to_broadcast([128, 4, D]), op=ALU.mult)))(hf, b),
                                    waits=[evw], sig=True)
            fbar = S_.last("pe", "act", "dve", "pool")
            hT_next = []
            stage1 = {}
            stage2 = {}

            def fin_s0(t16):
                j = t16 % 4
                si = t16 % 2
                tok = slice(t16 * 128, (t16 + 1) * 128)
                evl = op("sp", (lambda d, s_: (lambda e: e.dma_start(out=d, in_=s_)))(Xb[j], x_d[b, tok, :]),
                         waits=list(fin_free[j]) + fbar, sig=f"ld_f{j}", dma=True)
                ev = None
                for half in range(2):
                    for uu in range(8):
                        ev = op("pe", (lambda o, l, r, s0, s1: (lambda e: e.matmul(o, l, r, start=s0, stop=s1)))(
                            stp[si][:, half * 512:(half + 1) * 512], catT[:, uu, tok], wout[:, uu, half * 512:(half + 1) * 512],
                            uu == 0, uu == 7),
                            waits=([evw, ev_wg] + fbar + list(state["st_free"][si]) if (uu == 0 and half == 0) else ()),
                            sig=(True if (uu == 7 and half == 1) else None))
                for _d in range(N_WARM):
                    op("pe", lambda e: e.matmul(pp[:, :], identb[:, :], wout[:, 0, 0:512], start=True, stop=True))
                return (evl, ev)

            def fin_s1(t16, evs):
                j = t16 % 4
                si = t16 % 2
                evl, ev = evs
                e2 = op("dve", (lambda j_, s_: (lambda e: e.scalar_tensor_tensor(
                    out=Xb[j_], in0=Xb[j_], scalar=ALPHA, in1=stp[s_][:, :], op0=ALU.mult, op1=ALU.add)))(j, si),
                    waits=[ev, evl] + fbar, sig=True)
                state["st_free"][si] = [e2]
                e3 = None
                for hf in range(2):
                    e3 = op("dve", (lambda j_, h_: (lambda e: e.bn_stats(stats4[j_][:, h_, :], Xb[j_][:, h_ * 512:(h_ + 1) * 512])))(j, hf),
                            waits=[e2], sig=(True if hf == 1 else None))
                e4 = op("dve", (lambda j_: (lambda e: e.bn_aggr(mv4[j_][:, :], stats4[j_][:, :, :])))(j), waits=[e3], sig=True)
                e5a = op("act", (lambda j_: (lambda e: e.activation(rstd4[j_][:, :], mv4[j_][:, 1:2], AF.Ln, bias=epst[:, 0:1])))(j),
                         waits=[e4], sig=True)
                e5 = op("act", (lambda j_: (lambda e: e.activation(rstd4[j_][:, :], rstd4[j_][:, :], AF.Exp, scale=-0.5)))(j),
                        waits=[e5a], sig=True)
                return e5

            def fin_s2(t16, e5):
                j = t16 % 4
                tok = slice(t16 * 128, (t16 + 1) * 128)
                e6 = op("dve", (lambda j_: (lambda e: e.tensor_scalar(Rb4[j_], Xb[j_], mv4[j_][:, 0:1], rstd4[j_][:, 0:1],
                                                                      op0=ALU.subtract, op1=ALU.mult)))(j),
                        waits=[e5] + list(fin_free[j]), sig=True)
                e8 = op("pool", (lambda j_: (lambda e: e.tensor_tensor(Rb4[j_], Rb4[j_], lng[:, :], op=ALU.mult)))(j),
                        waits=[e6], sig=True)
                return e8

            def fin_s3(t16, e8):
                j = t16 % 4
                tok = slice(t16 * 128, (t16 + 1) * 128)
                e9 = op("dve", (lambda j_: (lambda e: e.tensor_tensor(Xb[j_], Rb4[j_], lnb[:, :], op=ALU.add)))(j),
                        waits=[e8], sig=True)
                evo = op("sp", (lambda s_, j_: (lambda e: e.dma_start(out=s_, in_=Xb[j_])))(y_d[b, tok, :], j),
                         waits=[e9], sig=f"st_y{j}", dma=True)
                fin_free[j] = [evo]
                return evo

            stage3 = {}
            for it in range(16 + 3):
                if it < 16:
                    stage1[it] = fin_s0(it)
                if 1 <= it <= 16:
                    stage2[it - 1] = fin_s1(it - 1, stage1[it - 1])
                if 2 <= it <= 17:
                    stage3[it - 2] = fin_s2(it - 2, stage2[it - 2])
                if it >= 3:
                    ev_out = fin_s3(it - 3, stage3[it - 3])
                if b + 1 < BPC and it < 16:
                    hT_chunk(b + 1, 2 * it, fbar)
                    hT_next = [hT_chunk(b + 1, 2 * it + 1, fbar)]
                if it == 15:
                    fin_pe_done = S_.last("pe")
                    state["wb_free"] = [list(fin_pe_done), list(fin_pe_done)]
                    if b + 1 < BPC:
                        load_weights(b + 1, 0)
                        load_weights(b + 1, 1)
            hT_ready = hT_next
        fin = S_.last("st_y0", "st_y1", "st_y2", "st_y3")
        op("sp", lambda e: e.nop(), waits=fin)

        with nc.Block() as block:
            @block.tensor
            def _(e):
                S_.replay("pe", e)

            @block.scalar
            def _(e):
                S_.replay("act", e)

            @block.vector
            def _(e):
                S_.replay("dve", e)

            @block.gpsimd
            def _(e):
                S_.replay("pool", e)

            @block.sync
            def _(e):
                S_.replay("sp", e)
    return nc


def _consts():
    f32 = np.float32
    half = 32
    inv = np.power(f32(10000.0), -(np.arange(half, dtype=f32) / f32(half))).astype(f32)
    pos = np.arange(S, dtype=f32)
    ang = (pos[:, None] * inv[None, :]).astype(f32)
    cos = np.cos(ang).astype(f32).T
    sin = np.sin(ang).astype(f32).T
    cos64 = np.concatenate([cos, cos], 0)
    sin64 = np.concatenate([-sin, sin], 0)
    cosT = np.ascontiguousarray(np.concatenate([cos64, cos64], 0))
    sinT = np.ascontiguousarray(np.concatenate([sin64, sin64], 0))
    c = np.arange(MASKW)[None, :]
    p = np.arange(128)[:, None]
    dl = c - 384 - p
    m = ((dl >= 0) & (dl <= 128)).astype(f32) + ((dl >= 0) & (dl % 4 == 0) & (dl <= 512)).astype(f32) \
        + ((dl >= 0) & (dl % 16 == 0) & (dl <= 2048)).astype(f32)
    maskA = m.astype(ml_dtypes.bfloat16)
    i = np.arange(128)[None, :]
    tri = (i >= p).astype(f32).astype(ml_dtypes.bfloat16)
    perm = np.zeros((128, 128), f32)
    for mm in range(128):
        partner = mm + 32 if (mm % 64) < 32 else mm - 32
        perm[partner, mm] = 1.0
    perm = perm.astype(ml_dtypes.bfloat16)
    identf = np.eye(128, dtype=f32)
    identb = np.eye(128, dtype=f32).astype(ml_dtypes.bfloat16)
    E8 = np.zeros((8, S), f32)
    for nn in range(8):
        E8[nn, nn * 256:(nn + 1) * 256] = 1.0
    E8 = E8.astype(ml_dtypes.bfloat16)
    gconst = np.zeros((128, 3, 16, 8), f32)
    for q16 in range(16):
        own = q16 // 2
        for nn in range(8):
            past = nn < own
            gconst[:, 0, q16, nn] = 0.0 if past else -1e30
            gconst[:, 1, q16, nn] = 1.0 if past else 0.0
            gconst[:, 2, q16, nn] = 1.0 if nn == own else 0.0
    gconst = gconst.reshape(128, 3, 128)
    sel2 = np.zeros((2, 2, 128), f32)
    sel2[0, 0, :] = 1.0
    sel2[1, 1, :] = 1.0
    return dict(cosT=cosT, sinT=sinT, maskA=maskA, tri=tri, perm=perm, identf=identf, identb=identb,
                E8=E8, gconst=gconst, sel2=sel2)


def kernel(x, c, w_in, w_out, w_ada, b_ada, ln_g, ln_b):
    x = np.asarray(x, dtype=np.float32)
    c = np.asarray(c, dtype=np.float32)
    w_in = np.ascontiguousarray(np.asarray(w_in, dtype=np.float32)[0])
    w_out = np.ascontiguousarray(np.asarray(w_out, dtype=np.float32)[0])
    w_ada = np.ascontiguousarray(np.asarray(w_ada, dtype=np.float32)[0])
    b_ada = np.asarray(b_ada, dtype=np.float32)[0]
    ln_g = np.asarray(ln_g, dtype=np.float32)[0]
    ln_b = np.asarray(ln_b, dtype=np.float32)[0]
    consts = _consts()
    shared = dict(consts)
    shared.update(
        w_in=w_in, w_out=w_out, w_ada=w_ada,
        bada2=np.ascontiguousarray(np.broadcast_to(b_ada[None, :], (2, 3072))),
        lng=np.ascontiguousarray(np.broadcast_to(ln_g[None, :], (128, D))),
        lnb=np.ascontiguousarray(np.broadcast_to(ln_b[None, :], (128, D))),
    )
    in_maps = []
    for k in range(NCORES):
        xb = x[k * BPC:(k + 1) * BPC]
        cb = c[k * BPC:(k + 1) * BPC]
        cT = np.ascontiguousarray(cb.reshape(BPC, 8, 128).transpose(2, 1, 0))
        m = dict(shared)
        m["x"] = np.ascontiguousarray(xb)
        m["xT"] = np.ascontiguousarray(xb.transpose(0, 2, 1))
        m["cT"] = cT
        in_maps.append(m)
    nc = build_program()
    res = run_bass_kernel_spmd(nc, in_maps, core_ids=list(range(NCORES)))
    out = np.concatenate([np.asarray(r["y"], dtype=np.float32) for r in res.results], axis=0)
    return out
```

```python
import numpy as np
import ml_dtypes
from contextlib import ExitStack
import concourse.bass as bass
import concourse.mybir as mybir
from concourse.bass_utils import run_bass_kernel_spmd

F32 = mybir.dt.float32
BF16 = mybir.dt.bfloat16
AF = mybir.ActivationFunctionType
ALU = mybir.AluOpType
AX = mybir.AxisListType

NCORES = 8
BPC = 2
S = 2048
D = 1024
BIG = 480.0
ALPHA = 2.0 ** 0.25
LN_EPS = 1e-5
MASKW = 2816
N_WARM = 0


class Sched:
    def __init__(self, nc, stack):
        self.nc = nc
        self.stack = stack
        self.streams = {k: [] for k in ("pe", "act", "dve", "pool", "sp")}
        self.sems = {}
        self.cnt = {}
        self.waited = {k: {} for k in self.streams}

    def _sem(self, name):
        if name not in self.sems:
            self.sems[name] = self.stack.enter_context(self.nc.semaphore(name))
            self.cnt[name] = 0
        return self.sems[name]

    def op(self, eng, fn, waits=(), sig=None, dma=False):
        for w in waits:
            if w is None:
                continue
            name, val = w
            if self.waited[eng].get(name, 0) >= val:
                continue
            self.waited[eng][name] = val
            self.streams[eng].append(("w", name, val))
        ev = None
        if sig is True:
            sig = eng
        if sig is not None:
            self._sem(sig)
            self.cnt[sig] += 16 if dma else 1
            ev = (sig, self.cnt[sig])
        self.streams[eng].append(("o", fn, sig, 16 if dma else 1))
        return ev

    def last(self, *names):
        return [(n, self.cnt[n]) for n in names if n in self.cnt and self.cnt[n] > 0]

    def replay(self, name, e):
        for item in self.streams[name]:
            if item[0] == "w":
                e.wait_ge(self.sems[item[1]], item[2])
            else:
                ins = item[1](e)
                if item[2] is not None:
                    ins.then_inc(self.sems[item[2]], item[3])


def build_program():
    nc = bass.Bass("TRN2", target_bir_lowering=False)

    def din(name, shape, dt=F32):
        return nc.dram_tensor(name, list(shape), dt, kind="ExternalInput").ap()

    xT_d = din("xT", [BPC, D, S])
    x_d = din("x", [BPC, S, D])
    cT_d = din("cT", [128, 8, BPC])
    win_d = din("w_in", [D, 4096])
    wout_d = din("w_out", [D, D])
    wada_d = din("w_ada", [D, 3072])
    bada_d = din("bada2", [2, 3072])
    lng_d = din("lng", [128, D])
    lnb_d = din("lnb", [128, D])
    cos_d = din("cosT", [128, S])
    sin_d = din("sinT", [128, S])
    maskA_d = din("maskA", [128, MASKW], BF16)
    tri_d = din("tri", [128, 128], BF16)
    perm_d = din("perm", [128, 128], BF16)
    identf_d = din("identf", [128, 128])
    identb_d = din("identb", [128, 128], BF16)
    e8_d = din("E8", [8, S], BF16)
    gconst_d = din("gconst", [128, 3, 128])
    sel2_d = din("sel2", [2, 2, 128])
    y_d = nc.dram_tensor("y", [BPC, S, D], F32, kind="ExternalOutput").ap()

    with ExitStack() as st:
        def sb(name, shape, dt):
            return st.enter_context(nc.sbuf_tensor(name, list(shape), dt))

        def pst(name, shape, dt=F32):
            return st.enter_context(nc.psum_tensor(name, list(shape), dt))

        cosT = sb("cosT_sb", [128, S], F32)
        sinT = sb("sinT_sb", [128, S], F32)
        maskA = sb("maskA_sb", [128, MASKW], BF16)
        tri = sb("tri_sb", [128, 128], BF16)
        perm = sb("perm_sb", [128, 128], BF16)
        identf = sb("identf_sb", [128, 128], F32)
        identb = sb("identb_sb", [128, 128], BF16)
        gconst = sb("gconst_sb", [128, 3, 128], F32)
        sel2 = sb("sel2_sb", [2, 2, 128], F32)
        cT = sb("cT_sb", [128, 8, BPC], F32)
        lng = sb("lng_sb", [128, D], F32)
        lnb = sb("lnb_sb", [128, D], F32)
        gate_bc = sb("gate_bc", [128, BPC, D], F32)
        sc = sb("sc_sb", [128, 16, BPC], F32)
        modp = [sb(f"modp{i}", [2, 512], F32) for i in range(2)]
        hT = sb("hT", [128, 8, S], BF16)
        catT = sb("catT", [128, 8, S], BF16)
        xs = [sb(f"xs{i}", [128, 512], F32) for i in range(2)]
        wball = sb("wball", [128, 8192], BF16)
        QT = [[sb(f"QT{i}{h}", [128, S], BF16) for h in range(2)] for i in range(2)]
        KT = [[sb(f"KT{i}{h}", [128, S], BF16) for h in range(2)] for i in range(2)]
        Vt = [sb(f"Vt{i}", [128, 16, 2, 128], BF16) for i in range(2)]
        pbuf = [sb(f"pbuf{i}", [128, 2, 512], BF16) for i in range(3)]
        qb = [sb(f"qb{i}", [128, 512], BF16) for i in range(2)]
        tmpf = sb("tmpf", [128, 2048], F32)
        rcp = [sb(f"rcp{i}", [128, 512], F32) for i in range(2)]
        kms2 = sb("kms2", [128, 2, 8], F32)
        kmhl2 = sb("kmhl2", [128, 2, 16], BF16)
        g16_2 = sb("g16_2", [128, 2, 256], F32)
        gate2 = sb("gate2", [128, 2, 16, 8], F32)
        top8_2 = sb("top8_2", [128, 2, 16, 8], F32)
        selt2 = sb("selt2", [128, 2, 16, 8], F32)
        nmpad2 = sb("nmpad2", [128, 2, 16, 72], BF16)
        epst = sb("epst", [128, 1], F32)
        onet = sb("onet", [128, 1], F32)

        stp = [pst(f"st{i}", [128, 1024]) for i in range(2)]
        otp = [pst(f"ot{i}", [128, 512]) for i in range(2)]
        pp = pst("pp", [128, 512])
        pr = pst("pr", [128, 512])

        wbuf = [wball[:, i * 4096:(i + 1) * 4096].rearrange("p (m s f) -> p m s f", m=8, s=4) for i in range(2)]
        wout = wball[:, :].rearrange("p (m f) -> p m f", m=8)
        t1 = [tmpf[:, 0:512], tmpf[:, 512:1024]]
        t2 = [tmpf[:, 1024:1536], tmpf[:, 1536:2048]]
        rbuf = [tmpf[:, 0:1024], tmpf[:, 1024:2048]]
        sgt = tmpf[:, 0:512]

        S_ = Sched(nc, st)
        op = S_.op

        badaf = wball[0:2, 0:6144].bitcast(F32)
        ev_bada = op("sp", lambda e: e.dma_start(out=badaf, in_=bada_d), sig="ld_bada", dma=True)
        evc = None
        for dst, src in ((cosT, cos_d), (sinT, sin_d), (maskA, maskA_d), (tri, tri_d), (perm, perm_d),
                         (identf, identf_d), (identb, identb_d), (gconst, gconst_d), (sel2, sel2_d),
                         (cT, cT_d), (lng, lng_d), (lnb, lnb_d)):
            evc = op("sp", (lambda d, s: (lambda e: e.dma_start(out=d[:], in_=s)))(dst, src), sig="ld_c", dma=True)
        ev_ms = None
        for i in range(2):
            ev_ms = op("pool", (lambda t: (lambda e: e.memset(t[:], 1.0)))(Vt[i]), sig=True)
        ev_ms = op("pool", lambda e: e.memset(nmpad2[:], 0.0), sig=True)
        ev_ms = op("pool", lambda e: e.memset(epst[:], LN_EPS), sig=True)
        ev_ms = op("pool", lambda e: e.memset(onet[:], 1.0), sig=True)

        xs_free = [None, None]
        modp_free = [None, None]
        ldn = 0
        tr_idx = 0
        ev_pr_free = None
        ev_st_free = [None, None]
        ev_sc = None
        ev_gate = None
        last_tr = None
        wreg = [hT[:, :, :].rearrange("p a b -> p (a b)").bitcast(F32).rearrange("p (m f) -> p m f", m=8),
                catT[:, :, :].rearrange("p a b -> p (a b)").bitcast(F32).rearrange("p (m f) -> p m f", m=8)]
        wreg_free = [None, None]
        wld = {}

        def wada_load(third):
            r = 0 if third == 0 else 1
            src = wada_d[:, third * 1024:(third + 1) * 1024].rearrange("(m p) f -> p m f", p=128)
            wld[third] = op("sp", (lambda d, s_: (lambda e: e.dma_start(out=d, in_=s_)))(wreg[r], src),
                            waits=[wreg_free[r]], sig=f"ld_wa{r}", dma=True)

        wada_load(0)
        wada_load(1)
        Xst = [QT[i][h][:, :].bitcast(F32) for i in range(2) for h in range(2)] + \
              [KT[i][h][:, :].bitcast(F32) for i in range(2) for h in range(2)]
        xst_ld = []
        for idx in range(8):
            mc, half = idx // 2, idx % 2
            xst_ld.append(op("sp", (lambda d, s_: (lambda e: e.dma_start(out=d, in_=s_)))(
                Xst[idx], xT_d[0, mc * 128:(mc + 1) * 128, half * 1024:(half + 1) * 1024]), sig=f"ld_xs{idx}", dma=True))
        for g6 in range(6):
            third, hf = g6 // 2, g6 % 2
            r = 0 if third == 0 else 1
            ev_mm = None
            for mc in range(8):
                last = (mc == 7)
                ev_mm = op("pe", (lambda l, r_, s0, s1: (lambda e: e.matmul(pp[0:2, :], l, r_, start=s0, stop=s1)))(
                    cT[:, mc, :], wreg[r][:, mc, hf * 512:(hf + 1) * 512], mc == 0, last),
                    waits=[wld[third], evc, S_.last("dve")[0] if (mc == 0 and S_.last("dve")) else None],
                    sig=(True if last else None))
            if hf == 1:
                wreg_free[r] = ev_mm
                if third == 1:
                    wada_load(2)
            mi = g6 % 2
            addc = 1.0 if g6 in (2, 3) else 0.0
            ev_mod = op("dve", (lambda m, a, g: (lambda e: e.scalar_tensor_tensor(
                out=m[0:2, :], in0=pp[0:2, :], scalar=a, in1=badaf[0:2, g * 512:(g + 1) * 512],
                op0=ALU.add, op1=ALU.add)))(modp[mi], addc, g6),
                waits=[ev_mm, modp_free[mi], evc, ev_bada], sig=True)
            if g6 < 4:
                for c4 in range(4):
                    idx = g6 * 4 + c4
                    last_tr = op("pe", (lambda m, c, ix: (lambda e: e.matmul(
                        pr[:, ix * 2:(ix + 1) * 2], m[0:2, c * 128:(c + 1) * 128], identf[0:2, 0:2],
                        start=True, stop=True)))(modp[mi], c4, idx),
                        waits=[ev_mod], sig=True)
                modp_free[mi] = last_tr
                if g6 == 3:
                    ev_sc = op("dve", lambda e: e.tensor_copy(
                        sc[:, :, :], pr[:, 0:32].rearrange("p (i b) -> p i b", b=2)),
                        waits=[last_tr], sig=True)
                    ev_pr_free = ev_sc
            else:
                for b in range(BPC):
                    ev_g = op("pe", (lambda m, bb: (lambda e: e.matmul(
                        stp[bb][:, 0:512], sel2[0:2, bb, :], m[0:2, :], start=True, stop=True)))(modp[mi], b),
                        waits=[ev_mod, ev_st_free[b]], sig=True)
                    ev_gate = op("dve", (lambda bb, g: (lambda e: e.tensor_copy(
                        gate_bc[:, bb, (g - 4) * 512:(g - 3) * 512], stp[bb][:, 0:512])))(b, g6),
                        waits=[ev_g], sig=True)
                    ev_st_free[b] = ev_gate
                    modp_free[mi] = ev_g

        state = {
            "pp_free": [ev_sc], "pr_free": [ev_sc],
            "qb_free": [None, None], "t_free": [None, None],
            "st_free": [[ev_st_free[0]], [ev_st_free[1]]],
            "ot_free": [[], []], "pb_free": [[], [], []],
            "wb_free": [[], []], "wb_ld": [None, None],
            "qk_free": [[], []], "vt_free": [[], []],
            "nm_free": [],
        }
        setup_done = [ev_sc, ev_gate, ev_ms, evc]

        def unit_cols(u):
            base = 0 if u < 4 else 2048
            uu = u % 4
            return [base + s * 512 + uu * 128 for s in range(4)]

        def load_weights(b, u, extra_waits=()):
            wb = u % 2
            cols = unit_cols(u)
            ev = None
            for s in range(4):
                src = win_d[:, cols[s]:cols[s] + 128].rearrange("(m p) f -> p m f", p=128)
                ev = op("pool", (lambda d, s_: (lambda e: e.dma_start(out=d, in_=s_)))(wbuf[wb][:, :, s, :], src),
                        waits=list(state["wb_free"][wb]) + list(extra_waits), sig=f"ld_w{wb}", dma=True)
            state["wb_ld"][wb] = ev
            state["wb_free"][wb] = []

        def proj_unit(b, u, hT_ready, full=True, silu_u=None):
            wb = u % 2
            qi = u % 2
            is_b = u >= 4
            first_waits = list(state["qk_free"][qi]) + list(state["vt_free"][qi])
            state["qk_free"][qi] = []
            state["vt_free"][qi] = []
            extra_ready = []
            if u in (4, 5):
                extra_ready.append(op("sp", lambda e: e.dma_start(out=KT[qi][0][64:72, :], in_=e8_d),
                                      waits=first_waits, sig=f"ld_e8_u{u}", dma=True))
            Pb = [[stp[0][:, 0:512], list(state["st_free"][0])], [stp[0][:, 512:1024], list(state["st_free"][0])],
                  [stp[1][:, 0:512], list(state["st_free"][1])], [stp[1][:, 512:1024], list(state["st_free"][1])]]
            Rb = [[pr[:, :], list(state["pr_free"])], [pp[:, :], list(state["pp_free"])],
                  [otp[0][:, :], list(state["ot_free"][0])], [otp[1][:, :], list(state["ot_free"][1])]]
            cnt = {"p": 0, "r": 0, "k": 0, "g": 0}
            last_pe = [None]

            def mm_group(bank, s, tok):
                ev = None
                for mc in range(8):
                    ev = op("pe", (lambda o, l, r, s0, s1: (lambda e: e.matmul(o, l, r, start=s0, stop=s1)))(
                        bank[0], wbuf[wb][:, mc, s, :], hT[:, mc, tok], mc == 0, mc == 7),
                        waits=([state["wb_ld"][wb]] + list(hT_ready) + list(bank[1]) if mc == 0 else ()),
                        sig=(True if mc == 7 else None))
                last_pe[0] = ev
                return ev

            def qk_step(s, tt):
                tok = slice(tt * 512, (tt + 1) * 512)
                bank = Pb[cnt["p"] % 4]
                cnt["p"] += 1
                rbank = Rb[cnt["r"] % 4]
                cnt["r"] += 1
                k = cnt["k"] % 2
                cnt["k"] += 1
                dst = (QT if s == 0 else KT)[qi]
                ev = mm_group(bank, s, tok)
                ev_a = op("act", lambda e: e.activation(qb[k][:, :], bank[0], AF.Copy),
                          waits=[ev, state["qb_free"][k]], sig=True)
                ev_t1 = op("dve", lambda e: e.tensor_tensor(t1[k], bank[0], cosT[:, tok], op=ALU.mult),
                           waits=[ev, ev_a, state["t_free"][k]], sig=True)
                bank[1] = [ev_a, ev_t1]

                def later():
                    ev_p = op("pe", lambda e: e.matmul(rbank[0], perm[:, :], qb[k][:, :], start=True, stop=True),
                              waits=[ev_a] + list(rbank[1]), sig=True)
                    state["qb_free"][k] = ev_p
                    ev_t2 = op("dve", lambda e: e.tensor_tensor(t2[k], rbank[0], sinT[:, tok], op=ALU.mult),
                               waits=[ev_p, ev_t1], sig=True)
                    rbank[1] = [ev_t2]
                    if not is_b:
                        ev_lo = op("dve", lambda e: e.tensor_tensor(dst[0][:, tok], t1[k], t2[k], op=ALU.add),
                                   waits=[ev_t2] + first_waits, sig=True)
                    else:
                        ev_lo = op("pool", lambda e: e.tensor_tensor(dst[0][0:64, tok], t1[k][0:64, :], t2[k][0:64, :], op=ALU.add),
                                   waits=[ev_t2] + first_waits, sig=True)
                        op("dve", lambda e: e.tensor_tensor(dst[1][0:64, tok], t1[k][64:128, :], t2[k][64:128, :], op=ALU.add),
                           waits=[ev_t2] + first_waits, sig=True)
                    state["t_free"][k] = ev_lo
                return later

            def g_step(tt):
                tok = slice(tt * 512, (tt + 1) * 512)
                bank = Pb[cnt["p"] % 4]
                cnt["p"] += 1
                gi = cnt["g"] % 2
                cnt["g"] += 1
                sg = rcp[gi][:, :]
                ev = mm_group(bank, 3, tok)

                def later():
                    ea = op("act", lambda e: e.activation(sg, bank[0], AF.Exp, scale=-1.0),
                            waits=[ev] + list(state["rc_free"][gi]), sig=True)
                    ea = op("act", lambda e: e.activation(sg, sg, AF.Ln, bias=onet[:, 0:1]), waits=[ea], sig=True)
                    ea = op("act", lambda e: e.activation(sg, sg, AF.Exp, scale=-1.0), waits=[ea], sig=True)
                    ev_d = op("dve", lambda e: e.tensor_tensor(catT[:, u, tok], bank[0], sg, op=ALU.mult),
                              waits=[ea, ev] + first_waits, sig=True)
                    bank[1] = [ev_d]
                    state["rc_free"][gi] = [ev_d]
                return later

            def v_step(kg):
                rbank = Rb[cnt["r"] % 4]
                cnt["r"] += 1
                ev = None
                for i4 in range(4):
                    kb = kg * 4 + i4
                    for mc in range(8):
                        ev = op("pe", (lambda o, l, r, s0, s1: (lambda e: e.matmul(o, l, r, start=s0, stop=s1)))(
                            rbank[0][:, i4 * 128:(i4 + 1) * 128], hT[:, mc, kb * 128:(kb + 1) * 128], wbuf[wb][:, mc, 2, :],
                            mc == 0, mc == 7),
                            waits=([state["wb_ld"][wb]] + list(hT_ready) + list(rbank[1]) if (mc == 0 and i4 == 0) else ()),
                            sig=(True if (mc == 7 and i4 == 3) else None))
                last_pe[0] = ev

                def later():
                    prv = rbank[0].rearrange("p (i h d) -> p i h d", i=4, h=2)
                    op("act", lambda e: e.activation(Vt[qi][:, kg * 4:(kg + 1) * 4, 0, 0:64], prv[:, :, 0, :], AF.Copy),
                       waits=[ev] + first_waits, sig=True)
                    ev1 = op("act", lambda e: e.activation(Vt[qi][:, kg * 4:(kg + 1) * 4, 1, 64:128], prv[:, :, 1, :], AF.Copy),
                             waits=[ev] + first_waits, sig=True)
                    rbank[1] = [ev1]
                return later

            kmean_ev = [None]

            def kmean_step():
                kts = KT[qi]
                kdone = S_.last("dve", "pool")
                e1 = [op("dve", (lambda h: (lambda e: e.tensor_reduce(
                    out=kms2[0:64, h, :], in_=kts[h][0:64, :].rearrange("p (n k) -> p n k", k=256), axis=AX.X, op=ALU.add)))(hh),
                    waits=kdone + list(state["nm_free"]), sig=True) for hh in range(2)]
                e2 = op("dve", lambda e: e.tensor_scalar(kmhl2[0:64, :, 0:8], kms2[0:64, :, :], 1.0 / 256.0, None, op0=ALU.mult),
                        waits=e1, sig=True)
                kmean_ev[0] = op("dve", lambda e: e.scalar_tensor_tensor(
                    out=kmhl2[0:64, :, 8:16], in0=kms2[0:64, :, :], scalar=1.0 / 256.0, in1=kmhl2[0:64, :, 0:8],
                    op0=ALU.mult, op1=ALU.subtract), waits=[e2], sig=True)
                return None

            def flush_step():
                return None

            def silu_step(tt):
                tok = slice(tt * 512, (tt + 1) * 512)
                gi = cnt["g"] % 2
                cnt["g"] += 1
                sg = rcp[gi][:, :]
                su = silu_u
                ea = op("act", lambda e: e.activation(sg, catT[:, su, tok], AF.Exp, scale=-1.0),
                        waits=list(state["rc_free"][gi]) + list(state["fill_done"]), sig=True)
                ea = op("act", lambda e: e.activation(sg, sg, AF.Ln, bias=onet[:, 0:1]), waits=[ea], sig=True)
                ea = op("act", lambda e: e.activation(sg, sg, AF.Exp, scale=-1.0), waits=[ea], sig=True)
                ev_d = op("dve", lambda e: e.tensor_tensor(catT[:, su, tok], catT[:, su, tok], sg, op=ALU.mult),
                          waits=[ea], sig=True)
                state["rc_free"][gi] = [ev_d]
                return None

            if full:
                steps = [lambda tt=tt: qk_step(1, tt) for tt in range(4)] + [lambda kg=kg: v_step(kg) for kg in range(4)] \
                    + [lambda tt=tt: qk_step(0, tt) for tt in range(4)] + [lambda tt=tt: g_step(tt) for tt in range(4)]
            else:
                steps = []
                for tt in range(4):
                    steps += [lambda tt=tt: qk_step(1, tt)]
                    if silu_u is not None:
                        steps += [lambda tt=tt: silu_step(tt)]
                if silu_u is None:
                    steps += [flush_step]
                nk = len(steps)
                steps += [lambda tt=tt: qk_step(0, tt) for tt in range(4)]
            pending = None
            for stf in steps:
                fin = stf()
                if pending is not None:
                    pending()
                pending = fin
                yield
            if pending is not None:
                pending()
            state["wb_free"][wb] = [last_pe[0]]
            state["pr_free"], state["pp_free"] = list(Rb[0][1]), list(Rb[1][1])
            state["ot_free"] = [list(Rb[2][1]), list(Rb[3][1])]
            state["st_free"] = [list(Pb[0][1]) + list(Pb[1][1]), list(Pb[2][1]) + list(Pb[3][1])]
            state["qk_done"] = S_.last("dve", "pool", "act") + extra_ready
            state["kmean_ev"] = kmean_ev[0]

        def silu_only(b, u):
            for tt in range(4):
                tok = slice(tt * 512, (tt + 1) * 512)
                gi = tt % 2
                sg = rcp[gi][:, :]
                ea = op("act", (lambda sg_, tk: (lambda e: e.activation(sg_, catT[:, u, tk], AF.Exp, scale=-1.0)))(sg, tok),
                        waits=list(state["rc_free"][gi]) + list(state["fill_done"]), sig=True)
                ea = op("act", (lambda sg_: (lambda e: e.activation(sg_, sg_, AF.Ln, bias=onet[:, 0:1])))(sg), waits=[ea], sig=True)
                ea = op("act", (lambda sg_: (lambda e: e.activation(sg_, sg_, AF.Exp, scale=-1.0)))(sg), waits=[ea], sig=True)
                ev_d = op("dve", (lambda sg_, tk: (lambda e: e.tensor_tensor(catT[:, u, tk], catT[:, u, tk], sg_, op=ALU.mult)))(sg, tok),
                          waits=[ea], sig=True)
                state["rc_free"][gi] = [ev_d]

        def gating_gen(b, u):
            qi = u % 2
            rope_done = list(state["qk_done"])
            qts, kts = QT[qi], KT[qi]
            gc = gconst[:, :, :].rearrange("p t (a n) -> p t a n", n=8)
            e1 = []
            for hh in range(2):
                e1.append(op("dve", (lambda h: (lambda e: e.tensor_reduce(
                    out=kms2[0:64, h, :], in_=kts[h][0:64, :].rearrange("p (n k) -> p n k", k=256), axis=AX.X, op=ALU.add)))(hh),
                    waits=rope_done + list(state["nm_free"]), sig=True))
                yield
            e2 = op("dve", lambda e: e.tensor_scalar(kmhl2[0:64, :, 0:8], kms2[0:64, :, :], 1.0 / 256.0, None, op0=ALU.mult),
                    waits=e1, sig=True)
            yield
            e3 = op("dve", lambda e: e.scalar_tensor_tensor(
                out=kmhl2[0:64, :, 8:16], in0=kms2[0:64, :, :], scalar=1.0 / 256.0, in1=kmhl2[0:64, :, 0:8],
                op0=ALU.mult, op1=ALU.subtract), waits=[e2], sig=True)
            yield
            evg = []
            for hh in range(2):
                ev = None
                for q16 in range(16):
                    ev = op("pe", (lambda q, h: (lambda e: e.matmul(
                        pp[:, h * 256 + q * 16:h * 256 + (q + 1) * 16], qts[h][0:64, q * 128:(q + 1) * 128], kmhl2[0:64, h, :],
                        start=True, stop=True)))(q16, hh),
                        waits=([e3] + rope_done + list(state["pp_free"]) if (q16 == 0 and hh == 0) else ()),
                        sig=(True if q16 == 15 else None))
                evg.append(ev)
                yield
            yield
            e4 = [op("dve", lambda e: e.tensor_copy(g16_2[:, :, :], pp[:, :].rearrange("p (h c) -> p h c", h=2)),
                     waits=[evg[1]], sig=True)]
            state["pp_free"] = [e4[0]]
            yield
            g16v = g16_2[:, :, :].rearrange("p h (a c) -> p h a c", c=16)
            e5 = op("dve", lambda e: e.tensor_tensor(gate2[:, :, :, :], g16v[:, :, :, 0:8], g16v[:, :, :, 8:16], op=ALU.add),
                    waits=e4, sig=True)
            yield
            e6 = None
            for hh in range(2):
                e6 = op("dve", (lambda h: (lambda e: e.tensor_tensor(gate2[:, h, :, :], gate2[:, h, :, :], gc[:, 0, :, :], op=ALU.add)))(hh),
                        waits=[e5], sig=True)
            yield
            e7 = None
            for hh in range(2):
                for q16 in range(16):
                    e7 = op("dve", (lambda q, h: (lambda e: e.max(top8_2[:, h, q, :], gate2[:, h, q, :])))(q16, hh),
                            waits=[e6], sig=(True if (q16 == 15 and hh == 1) else None))
                yield
            e8 = op("dve", lambda e: e.tensor_tensor(selt2[:, :, :, :], gate2[:, :, :, :],
                                                     top8_2[:, :, :, 2:3].to_broadcast([128, 2, 16, 8]), op=ALU.is_ge),
                    waits=[e7], sig=True)
            yield
            e9 = None
            for hh in range(2):
                e9 = op("dve", (lambda h: (lambda e: e.tensor_tensor(selt2[:, h, :, :], selt2[:, h, :, :], gc[:, 1, :, :], op=ALU.mult)))(hh),
                        waits=[e8], sig=True)
            yield
            e10 = None
            for hh in range(2):
                e10 = op("dve", (lambda h: (lambda e: e.tensor_tensor(selt2[:, h, :, :], selt2[:, h, :, :], gc[:, 2, :, :], op=ALU.add)))(hh),
                         waits=[e9], sig=True)
            yield
            e11 = op("dve", lambda e: e.tensor_scalar(nmpad2[:, :, :, 64:72], selt2[:, :, :, :], BIG, -BIG,
                                                      op0=ALU.mult, op1=ALU.add),
                     waits=[e10] + list(state["nm_free"]), sig=True)
            yield
            n = 0
            ev = None
            for hh in range(2):
                for piece in range(4):
                    bank = pp
                    fkey = "pp_free"
                    n += 1
                    for i4 in range(4):
                        q16 = piece * 4 + i4
                        ev = op("pe", (lambda q, i_, h, bk: (lambda e: e.matmul(
                            bk[0:72, i_ * 128:(i_ + 1) * 128], nmpad2[:, h, q, 0:72], identb[:, :],
                            start=True, stop=True)))(q16, i4, hh, bank),
                            waits=([e11] + list(state[fkey]) if i4 == 0 else ()),
                            sig=(True if i4 == 3 else None))
                    yield
                    e12 = op("dve", (lambda pc, h, bk: (lambda e: e.tensor_copy(
                        qts[h][64:72, pc * 512:(pc + 1) * 512], bk[64:72, :])))(piece, hh, bank),
                        waits=[ev], sig=True)
                    state[fkey] = [e12]
                    yield
            state["nm_free"] = [ev]

        def vg_filler(b, u, single=False):
            wb = u % 2
            qi = u % 2
            fw = list(state["vt_free"][qi])
            banks = [[pr[:, :], "pr_free"]] if single else [[pr[:, :], "pr_free"], [pp[:, :], "pp_free"]]
            n = 0
            hT_ready = state["hT_ready"]
            for kind, idx in [("v", 0), ("g", 0), ("v", 1), ("g", 1), ("v", 2), ("g", 2), ("v", 3), ("g", 3)]:
                bank, fkey = banks[n % len(banks)]
                n += 1
                ev = None
                if kind == "v":
                    for i4 in range(4):
                        kb = idx * 4 + i4
                        for mc in range(8):
                            ev = op("pe", (lambda o, l, r, s0, s1: (lambda e: e.matmul(o, l, r, start=s0, stop=s1)))(
                                bank[:, i4 * 128:(i4 + 1) * 128], hT[:, mc, kb * 128:(kb + 1) * 128], wbuf[wb][:, mc, 2, :],
                                mc == 0, mc == 7),
                                waits=([state["wb_ld"][wb]] + list(hT_ready) + list(state[fkey]) if (mc == 0 and i4 == 0) else ()),
                                sig=(True if (mc == 7 and i4 == 3) else None))
                        if (not single) or i4 % 2 == 1:
                            yield
                    prv = bank.rearrange("p (i h d) -> p i h d", i=4, h=2)
                    op("dve", (lambda pv, kg: (lambda e: e.tensor_copy(Vt[qi][:, kg * 4:(kg + 1) * 4, 0, 0:64], pv[:, :, 0, :])))(prv, idx),
                       waits=[ev] + fw, sig=True)
                    ev1 = op("dve", (lambda pv, kg: (lambda e: e.tensor_copy(Vt[qi][:, kg * 4:(kg + 1) * 4, 1, 64:128], pv[:, :, 1, :])))(prv, idx),
                             waits=[ev] + fw, sig=True)
                    state[fkey] = [ev1]
                else:
                    tok = slice(idx * 512, (idx + 1) * 512)
                    for mc in range(8):
                        ev = op("pe", (lambda o, l, r, s0, s1: (lambda e: e.matmul(o, l, r, start=s0, stop=s1)))(
                            bank, wbuf[wb][:, mc, 3, :], hT[:, mc, tok], mc == 0, mc == 7),
                            waits=([state["wb_ld"][wb]] + list(hT_ready) + list(state[fkey]) if mc == 0 else ()),
                            sig=(True if mc == 7 else None))
                        if (mc % 2 == 1 and not single) or (mc % 4 == 3 and single):
                            yield
                    ev1 = op("dve", (lambda bk, tk: (lambda e: e.tensor_copy(catT[:, u, tk], bk)))(bank, tok),
                             waits=[ev], sig=True)
                    state[fkey] = [ev1]
                state["fill_done"] = [ev1]
                state["fill_last_pe"] = [ev]
                yield
                if single:
                    yield

        def att_unit(b, u, ready, qready=None):
            if qready is None:
                qready = ready
            qi = u % 2
            is_b = u >= 4
            K = 72 if is_b else 64
            groups = []
            for hh in range(2):
                for qt in range(4):
                    ng = 2 * qt + 2
                    for j in range(ng):
                        groups.append((hh, qt, j, j == ng - 1))
            n = len(groups)
            exp_ev = [None] * n
            msk_ev = [None] * n
            ot_of = {}
            last_norm = []
            for i in range(n + 2):
                if i < n:
                    hh, qt, j, lastg = groups[i]
                    sti = state["gcount"] % 2
                    pbi = state["gcount"] % 3
                    state["gcount"] += 1
                    if is_b:
                        qsl, ksl, rows = QT[qi][hh], KT[qi][hh], slice(0, K)
                    else:
                        qsl, ksl, rows = QT[qi][0], KT[qi][0], slice(64 * hh, 64 * hh + 64)
                    kbs = (2 * j, 2 * j + 1)
                    c0s = [max(0, (kb - 4 * qt) * 128) for kb in kbs]
                    ev = None
                    for jj in range(2):
                        kb, c0 = kbs[jj], c0s[jj]
                        ev = op("pe", (lambda o, l, r: (lambda e: e.matmul(o, l, r, start=True, stop=True)))(
                            stp[sti][:, jj * 512 + c0:(jj + 1) * 512], ksl[rows, kb * 128:(kb + 1) * 128],
                            qsl[rows, qt * 512 + c0:(qt + 1) * 512]),
                            waits=(list(qready) + list(state["st_free"][sti]) if jj == 0 else ()),
                            sig=(True if jj == 1 else None))
                    stv = stp[sti][:, :].rearrange("p (j c) -> p j c", j=2)
                    if c0s[0] == c0s[1]:
                        ev_e = op("act", (lambda pb_, sv, c: (lambda e: e.activation(
                            pb_[:, :, c:512], sv[:, :, c:512], AF.Exp, scale=0.125)))(pbuf[pbi], stv, c0s[0]),
                            waits=[ev] + list(state["pb_free"][pbi]), sig=True)
                    else:
                        for jj in range(2):
                            ev_e = op("act", (lambda pb_, sv, c, j_: (lambda e: e.activation(
                                pb_[:, j_, c:512], sv[:, j_, c:512], AF.Exp, scale=0.125)))(pbuf[pbi], stv, c0s[jj], jj),
                                waits=[ev] + list(state["pb_free"][pbi]), sig=True)
                    state["st_free"][sti] = [ev_e]
                    exp_ev[i] = (ev_e, pbi, c0s)
                    evs_m = []
                    for jj in range(2):
                        kb, c0 = kbs[jj], c0s[jj]
                        if not is_b:
                            o_ = 4 * qt - kb
                            base = 128 * (o_ + 3)
                            eng = "dve"
                            evs_m.append(op(eng, (lambda pb_, j_, c, bs: (lambda e: e.tensor_tensor(
                                pb_[:, j_, c:512], pb_[:, j_, c:512], maskA[:, bs + c:bs + 512], op=ALU.mult)))(
                                pbuf[pbi], jj, c0, base), waits=[ev_e], sig=True))
                        elif kb >= 4 * qt:
                            evs_m.append(op("dve", (lambda pb_, j_, c: (lambda e: e.tensor_tensor(
                                pb_[:, j_, c:c + 128], pb_[:, j_, c:c + 128], tri[:, :], op=ALU.mult)))(
                                pbuf[pbi], jj, c0), waits=[ev_e], sig=True))
                    msk_ev[i] = evs_m
                if i >= 2:
                    i2 = i - 2
                    hh, qt, j, lastg = groups[i2]
                    ev_e, pbi, c0s = exp_ev[i2]
                    key = (hh, qt)
                    if key not in ot_of:
                        ot_of[key] = state["ot_n"] % 2
                        state["ot_n"] += 1
                    oi = ot_of[key]
                    ev = None
                    for jj in range(2):
                        kb, c0 = 2 * j + jj, c0s[jj]
                        first = (j == 0 and jj == 0)
                        ev = op("pe", (lambda o, l, r, f: (lambda e: e.matmul(o, l, r, start=f, stop=False,
                                                                              skip_group_check=True)))(
                            otp[oi][:, c0:512], Vt[qi][:, kb, hh, :], pbuf[pbi][:, jj, c0:512], first),
                            waits=(list(msk_ev[i2]) + [ev_e] + (list(state["ot_free"][oi]) if first else []) if jj == 0 else ()),
                            sig=(True if jj == 1 else None))
                    state["pb_free"][pbi] = [ev]
                    if lastg:
                        tok = slice(qt * 512, (qt + 1) * 512)
                        ri = state["rc_n"] % 2
                        state["rc_n"] += 1
                        orow = slice(0, 64) if hh == 0 else slice(64, 128)
                        lrow = slice(64, 128) if hh == 0 else slice(0, 64)
                        e1a = op("act", (lambda r_, o_, orow_, lrow_: (lambda e: e.activation(r_[orow_, :], o_[lrow_, :], AF.Ln)))(
                            rcp[ri], otp[oi], orow, lrow), waits=[ev] + list(state["rc_free"][ri]), sig=True)
                        e1 = op("act", (lambda r_, orow_: (lambda e: e.activation(r_[orow_, :], r_[orow_, :], AF.Exp, scale=-1.0)))(
                            rcp[ri], orow), waits=[e1a], sig=True)
                        e2 = op("dve", (lambda r_, o_, orow_: (lambda e: e.tensor_tensor(
                            r_[orow_, :], o_[orow_, :], r_[orow_, :], op=ALU.mult)))(rcp[ri], otp[oi], orow),
                            waits=[e1], sig=True)
                        state["ot_free"][oi] = [e2]
                        e3 = op("pool", (lambda n_, orow_, tk: (lambda e: e.tensor_tensor(
                            catT[orow_, u, tk], n_[orow_, :], catT[orow_, u, tk], op=ALU.mult)))(rcp[ri], orow, tok),
                            waits=[e2] + list(ready), sig=True)
                        state["rc_free"][ri] = [e3]
                        last_norm = [e3]
                yield
            state["qk_free"][qi] = S_.last("pe")
            state["vt_free"][qi] = S_.last("pe")
            state["att_done"] = last_norm

        state["gcount"] = 0
        state["ot_n"] = 0
        state["rc_n"] = 0
        state["rc_free"] = [[], []]

        def run(gen):
            for _ in gen:
                pass

        def merge3(ga, gb, gc_):
            done_b = gb is None
            done_c = gc_ is None
            for _ in ga:
                if not done_b:
                    try:
                        next(gb)
                    except StopIteration:
                        done_b = True
                if not done_c:
                    try:
                        next(gc_)
                    except StopIteration:
                        done_c = True
            if not done_b:
                for _ in gb:
                    pass
            if not done_c:
                for _ in gc_:
                    pass

        def merge(ga, gb, ratio):
            acc = 0.0
            gb_done = gb is None
            for _ in ga:
                acc += ratio
                while acc >= 1.0 and not gb_done:
                    acc -= 1.0
                    try:
                        next(gb)
                    except StopIteration:
                        gb_done = True
            if not gb_done:
                for _ in gb:
                    pass

        ev_out = None
        stats4 = [sb(f"stats4_{i}", [128, 2, 6], F32) for i in range(4)]
        mv4 = [sb(f"mv4_{i}", [128, 2], F32) for i in range(4)]
        rstd4 = [sb(f"rstd4_{i}", [128, 1], F32) for i in range(4)]
        Xb = [QT[i][h][:, :].bitcast(F32) for i in range(2) for h in range(2)]
        Rb4 = [KT[i][h][:, :].bitcast(F32) for i in range(2) for h in range(2)]
        fin_free = [[], [], [], []]
        fin_pe_done = []

        def hT_chunk(bn, idx, waits0):
            mc, q4 = idx // 4, idx % 4
            i = idx % 2
            src = xT_d[bn, mc * 128:(mc + 1) * 128, q4 * 512:(q4 + 1) * 512]
            evl = op("sp", (lambda d, s_: (lambda e: e.dma_start(out=d[:, :], in_=s_)))(xs[i], src),
                     waits=[xs_free[i]] + list(waits0), sig=f"ld_x{i}", dma=True)
            eva = op("act", (lambda d, mc_, h_, b_: (lambda e: e.activation(
                hT[:, mc_, h_ * 512:(h_ + 1) * 512], d[:, :], AF.Identity,
                bias=sc[:, mc_, b_:b_ + 1], scale=sc[:, 8 + mc_, b_:b_ + 1])))(xs[i], mc, q4, bn),
                waits=[evl] + list(waits0), sig=True)
            xs_free[i] = eva
            return eva

        hT_ready = []
        xst_free = [None] * 8
        for b in range(BPC):
            bar = S_.last("pe", "act", "dve", "pool") + (setup_done if b == 0 else [])
            if b == 0:
                for idx in range(16):
                    mc, half = idx // 2, idx % 2
                    sidx = idx % 8
                    if idx >= 8:
                        xst_ld[sidx] = op("sp", (lambda d, s_: (lambda e: e.dma_start(out=d, in_=s_)))(
                            Xst[sidx], xT_d[0, mc * 128:(mc + 1) * 128, half * 1024:(half + 1) * 1024]),
                            waits=[xst_free[sidx]], sig=f"ld_xs{sidx}", dma=True)
                    eva = op("act", (lambda d, mc_, h_: (lambda e: e.activation(
                        hT[:, mc_, h_ * 1024:(h_ + 1) * 1024], d, AF.Identity,
                        bias=sc[:, mc_, 0:1], scale=sc[:, 8 + mc_, 0:1])))(Xst[sidx], mc, half),
                        waits=[xst_ld[sidx], ev_sc], sig=True)
                    xst_free[sidx] = eva
                    hT_ready = [eva]
                for i in range(2):
                    for h in range(2):
                        extra = op("sp", (lambda d: (lambda e: e.dma_start(out=d[64:72, :], in_=e8_d)))(KT[i][h]),
                                   waits=hT_ready, sig="ld_e8", dma=True)
                bar = bar + [extra]
                load_weights(b, 0, extra_waits=bar)
                load_weights(b, 1, extra_waits=bar)
                state["qk_free"] = [list(bar), list(bar)]
            else:
                qf = []
                for i in range(2):
                    fr = list(fin_free[2 * i]) + list(fin_free[2 * i + 1]) + list(fin_pe_done)
                    ex = op("sp", (lambda d: (lambda e: e.dma_start(out=d[64:72, :], in_=e8_d)))(KT[i][1]),
                            waits=fr, sig=f"ld_e8b{i}", dma=True)
                    qf.append(fr + [ex])
                state["qk_free"] = qf
            state["hT_ready"] = hT_ready
            state["fill_done"] = []
            def e8_events():
                return [ev_ for ev_ in state["qk_done"] if ev_[0].startswith("ld_e8")]

            e8_all = []
            run(proj_unit(b, 0, hT_ready, full=True))
            load_weights(b, 2)
            qsnap = S_.last("dve", "pool", "act") + e8_all
            run(proj_unit(b, 1, hT_ready, full=False, silu_u=None))
            for u in range(8):
                ready = S_.last("dve", "pool", "act") + e8_all
                if u < 7:
                    gg = gating_gen(b, u + 1) if u + 1 >= 4 else None
                    merge3(att_unit(b, u, ready, qsnap), vg_filler(b, u + 1, single=(gg is not None)), gg)
                    qsnap = S_.last("dve", "pool", "act") + e8_all
                    state["wb_free"][(u + 1) % 2] = list(state["fill_last_pe"])
                    if u + 3 < 8:
                        load_weights(b, u + 3)
                    if u + 2 < 8:
                        run(proj_unit(b, u + 2, hT_ready, full=False, silu_u=u + 1))
                        e8_all += e8_events()
                    else:
                        evw = None
                        for half in range(2):
                            src = wout_d[:, half * 512:(half + 1) * 512].rearrange("(m p) f -> p m f", p=128)
                            evw = op("pool", (lambda s_, h_: (lambda e: e.dma_start(out=wout[:, :, h_ * 512:(h_ + 1) * 512], in_=s_)))(src, half),
                                     waits=list(state["wb_free"][0]) + list(state["wb_free"][1]), sig="ld_wo", dma=True)
                        silu_only(b, u + 1)
                else:
                    for i_att, _ in enumerate(att_unit(b, u, ready, qsnap)):
                        if i_att == 22:
                            for hf in range(2):
                                ev_wg = op("dve", (lambda h_, b_: (lambda e: e.tensor_tensor(
                                    wout[:, 4 * h_:4 * h_ + 4, :], wout[:, 4 * h_:4 * h_ + 4, :],
                                    gate_bc[:, b_:b_ + 1, :].to_broadcast([128, 4, D]), op=ALU.mult)))(hf, b),
                                    waits=[evw], sig=True)
            fbar = S_.last("pe", "act", "dve", "pool")
            hT_next = []
            stage1 = {}
            stage2 = {}

            def fin_s0(t16):
                j = t16 % 4
                si = t16 % 2
                tok = slice(t16 * 128, (t16 + 1) * 128)
                evl = op("sp", (lambda d, s_: (lambda e: e.dma_start(out=d, in_=s_)))(Xb[j], x_d[b, tok, :]),
                         waits=list(fin_free[j]) + fbar, sig=f"ld_f{j}", dma=True)
                ev = None
                for half in range(2):
                    for uu in range(8):
                        ev = op("pe", (lambda o, l, r, s0, s1: (lambda e: e.matmul(o, l, r, start=s0, stop=s1)))(
                            stp[si][:, half * 512:(half + 1) * 512], catT[:, uu, tok], wout[:, uu, half * 512:(half + 1) * 512],
                            uu == 0, uu == 7),
                            waits=([evw, ev_wg] + fbar + list(state["st_free"][si]) if (uu == 0 and half == 0) else ()),
                            sig=(True if (uu == 7 and half == 1) else None))
                for _d in range(N_WARM):
                    op("pe", lambda e: e.matmul(pp[:, :], identb[:, :], wout[:, 0, 0:512], start=True, stop=True))
                return (evl, ev)

            def fin_s1(t16, evs):
                j = t16 % 4
                si = t16 % 2
                evl, ev = evs
                e2 = op("dve", (lambda j_, s_: (lambda e: e.scalar_tensor_tensor(
                    out=Xb[j_], in0=Xb[j_], scalar=ALPHA, in1=stp[s_][:, :], op0=ALU.mult, op1=ALU.add)))(j, si),
                    waits=[ev, evl] + fbar, sig=True)
                state["st_free"][si] = [e2]
                e3 = None
                for hf in range(2):
                    e3 = op("dve", (lambda j_, h_: (lambda e: e.bn_stats(stats4[j_][:, h_, :], Xb[j_][:, h_ * 512:(h_ + 1) * 512])))(j, hf),
                            waits=[e2], sig=(True if hf == 1 else None))
                e4 = op("dve", (lambda j_: (lambda e: e.bn_aggr(mv4[j_][:, :], stats4[j_][:, :, :])))(j), waits=[e3], sig=True)
                e5a = op("act", (lambda j_: (lambda e: e.activation(rstd4[j_][:, :], mv4[j_][:, 1:2], AF.Ln, bias=epst[:, 0:1])))(j),
                         waits=[e4], sig=True)
                e5 = op("act", (lambda j_: (lambda e: e.activation(rstd4[j_][:, :], rstd4[j_][:, :], AF.Exp, scale=-0.5)))(j),
                        waits=[e5a], sig=True)
                return e5

            def fin_s2(t16, e5):
                j = t16 % 4
                tok = slice(t16 * 128, (t16 + 1) * 128)
                e6 = op("dve", (lambda j_: (lambda e: e.tensor_scalar(Rb4[j_], Xb[j_], mv4[j_][:, 0:1], rstd4[j_][:, 0:1],
                                                                      op0=ALU.subtract, op1=ALU.mult)))(j),
                        waits=[e5] + list(fin_free[j]), sig=True)
                e8 = op("pool", (lambda j_: (lambda e: e.tensor_tensor(Rb4[j_], Rb4[j_], lng[:, :], op=ALU.mult)))(j),
                        waits=[e6], sig=True)
                return e8

            def fin_s3(t16, e8):
                j = t16 % 4
                tok = slice(t16 * 128, (t16 + 1) * 128)
                e9 = op("dve", (lambda j_: (lambda e: e.tensor_tensor(Xb[j_], Rb4[j_], lnb[:, :], op=ALU.add)))(j),
                        waits=[e8], sig=True)
                evo = op("sp", (lambda s_, j_: (lambda e: e.dma_start(out=s_, in_=Xb[j_])))(y_d[b, tok, :], j),
                         waits=[e9], sig=f"st_y{j}", dma=True)
                fin_free[j] = [evo]
                return evo

            stage3 = {}
            for it in range(16 + 3):
                if it < 16:
                    stage1[it] = fin_s0(it)
                if 1 <= it <= 16:
                    stage2[it - 1] = fin_s1(it - 1, stage1[it - 1])
                if 2 <= it <= 17:
                    stage3[it - 2] = fin_s2(it - 2, stage2[it - 2])
                if it >= 3:
                    ev_out = fin_s3(it - 3, stage3[it - 3])
                if b + 1 < BPC and it < 16:
                    hT_chunk(b + 1, 2 * it, fbar)
                    hT_next = [hT_chunk(b + 1, 2 * it + 1, fbar)]
                if it == 15:
                    fin_pe_done = S_.last("pe")
                    state["wb_free"] = [list(fin_pe_done), list(fin_pe_done)]
                    if b + 1 < BPC:
                        load_weights(b + 1, 0)
                        load_weights(b + 1, 1)
            hT_ready = hT_next
        fin = S_.last("st_y0", "st_y1", "st_y2", "st_y3")
        op("sp", lambda e: e.nop(), waits=fin)

        with nc.Block() as block:
            @block.tensor
            def _(e):
                S_.replay("pe", e)

            @block.scalar
            def _(e):
                S_.replay("act", e)

            @block.vector
            def _(e):
                S_.replay("dve", e)

            @block.gpsimd
            def _(e):
                S_.replay("pool", e)

            @block.sync
            def _(e):
                S_.replay("sp", e)
    return nc


def _consts():
    f32 = np.float32
    half = 32
    inv = np.power(f32(10000.0), -(np.arange(half, dtype=f32) / f32(half))).astype(f32)
    pos = np.arange(S, dtype=f32)
    ang = (pos[:, None] * inv[None, :]).astype(f32)
    cos = np.cos(ang).astype(f32).T
    sin = np.sin(ang).astype(f32).T
    cos64 = np.concatenate([cos, cos], 0)
    sin64 = np.concatenate([-sin, sin], 0)
    cosT = np.ascontiguousarray(np.concatenate([cos64, cos64], 0))
    sinT = np.ascontiguousarray(np.concatenate([sin64, sin64], 0))
    c = np.arange(MASKW)[None, :]
    p = np.arange(128)[:, None]
    dl = c - 384 - p
    m = ((dl >= 0) & (dl <= 128)).astype(f32) + ((dl >= 0) & (dl % 4 == 0) & (dl <= 512)).astype(f32) \
        + ((dl >= 0) & (dl % 16 == 0) & (dl <= 2048)).astype(f32)
    maskA = m.astype(ml_dtypes.bfloat16)
    i = np.arange(128)[None, :]
    tri = (i >= p).astype(f32).astype(ml_dtypes.bfloat16)
    perm = np.zeros((128, 128), f32)
    for mm in range(128):
        partner = mm + 32 if (mm % 64) < 32 else mm - 32
        perm[partner, mm] = 1.0
    perm = perm.astype(ml_dtypes.bfloat16)
    identf = np.eye(128, dtype=f32)
    identb = np.eye(128, dtype=f32).astype(ml_dtypes.bfloat16)
    E8 = np.zeros((8, S), f32)
    for nn in range(8):
        E8[nn, nn * 256:(nn + 1) * 256] = 1.0
    E8 = E8.astype(ml_dtypes.bfloat16)
    gconst = np.zeros((128, 3, 16, 8), f32)
    for q16 in range(16):
        own = q16 // 2
        for nn in range(8):
            past = nn < own
            gconst[:, 0, q16, nn] = 0.0 if past else -1e30
            gconst[:, 1, q16, nn] = 1.0 if past else 0.0
            gconst[:, 2, q16, nn] = 1.0 if nn == own else 0.0
    gconst = gconst.reshape(128, 3, 128)
    sel2 = np.zeros((2, 2, 128), f32)
    sel2[0, 0, :] = 1.0
    sel2[1, 1, :] = 1.0
    return dict(cosT=cosT, sinT=sinT, maskA=maskA, tri=tri, perm=perm, identf=identf, identb=identb,
                E8=E8, gconst=gconst, sel2=sel2)


def kernel(x, c, w_in, w_out, w_ada, b_ada, ln_g, ln_b):
    x = np.asarray(x, dtype=np.float32)
    c = np.asarray(c, dtype=np.float32)
    w_in = np.ascontiguousarray(np.asarray(w_in, dtype=np.float32)[0])
    w_out = np.ascontiguousarray(np.asarray(w_out, dtype=np.float32)[0])
    w_ada = np.ascontiguousarray(np.asarray(w_ada, dtype=np.float32)[0])
    b_ada = np.asarray(b_ada, dtype=np.float32)[0]
    ln_g = np.asarray(ln_g, dtype=np.float32)[0]
    ln_b = np.asarray(ln_b, dtype=np.float32)[0]
    consts = _consts()
    shared = dict(consts)
    shared.update(
        w_in=w_in, w_out=w_out, w_ada=w_ada,
        bada2=np.ascontiguousarray(np.broadcast_to(b_ada[None, :], (2, 3072))),
        lng=np.ascontiguousarray(np.broadcast_to(ln_g[None, :], (128, D))),
        lnb=np.ascontiguousarray(np.broadcast_to(ln_b[None, :], (128, D))),
    )
    in_maps = []
    for k in range(NCORES):
        xb = x[k * BPC:(k + 1) * BPC]
        cb = c[k * BPC:(k + 1) * BPC]
        cT = np.ascontiguousarray(cb.reshape(BPC, 8, 128).transpose(2, 1, 0))
        m = dict(shared)
        m["x"] = np.ascontiguousarray(xb)
        m["xT"] = np.ascontiguousarray(xb.transpose(0, 2, 1))
        m["cT"] = cT
        in_maps.append(m)
    nc = build_program()
    res = run_bass_kernel_spmd(nc, in_maps, core_ids=list(range(NCORES)))
    out = np.concatenate([np.asarray(r["y"], dtype=np.float32) for r in res.results], axis=0)
    return out
```

```python
import numpy as np
import ml_dtypes
from contextlib import ExitStack
import concourse.bass as bass
import concourse.mybir as mybir
from concourse.bass_utils import run_bass_kernel_spmd

F32 = mybir.dt.float32
BF16 = mybir.dt.bfloat16
AF = mybir.ActivationFunctionType
ALU = mybir.AluOpType
AX = mybir.AxisListType

NCORES = 8
BPC = 2
S = 2048
D = 1024
BIG = 480.0
ALPHA = 2.0 ** 0.25
LN_EPS = 1e-5
MASKW = 2816
N_WARM = 0


class Sched:
    def __init__(self, nc, stack):
        self.nc = nc
        self.stack = stack
        self.streams = {k: [] for k in ("pe", "act", "dve", "pool", "sp")}
        self.sems = {}
        self.cnt = {}
        self.waited = {k: {} for k in self.streams}

    def _sem(self, name):
        if name not in self.sems:
            self.sems[name] = self.stack.enter_context(self.nc.semaphore(name))
            self.cnt[name] = 0
        return self.sems[name]

    def op(self, eng, fn, waits=(), sig=None, dma=False):
        for w in waits:
            if w is None:
                continue
            name, val = w
            if self.waited[eng].get(name, 0) >= val:
                continue
            self.waited[eng][name] = val
            self.streams[eng].append(("w", name, val))
        ev = None
        if sig is True:
            sig = eng
        if sig is not None:
            self._sem(sig)
            self.cnt[sig] += 16 if dma else 1
            ev = (sig, self.cnt[sig])
        self.streams[eng].append(("o", fn, sig, 16 if dma else 1))
        return ev

    def last(self, *names):
        return [(n, self.cnt[n]) for n in names if n in self.cnt and self.cnt[n] > 0]

    def replay(self, name, e):
        for item in self.streams[name]:
            if item[0] == "w":
                e.wait_ge(self.sems[item[1]], item[2])
            else:
                ins = item[1](e)
                if item[2] is not None:
                    ins.then_inc(self.sems[item[2]], item[3])


def build_program():
    nc = bass.Bass("TRN2", target_bir_lowering=False)

    def din(name, shape, dt=F32):
        return nc.dram_tensor(name, list(shape), dt, kind="ExternalInput").ap()

    xT_d = din("xT", [BPC, D, S])
    x_d = din("x", [BPC, S, D])
    cT_d = din("cT", [128, 8, BPC])
    win_d = din("w_in", [D, 4096])
    wout_d = din("w_out", [D, D])
    wada_d = din("w_ada", [D, 3072])
    bada_d = din("bada2", [2, 3072])
    lng_d = din("lng", [128, D])
    lnb_d = din("lnb", [128, D])
    cos_d = din("cosT", [128, S])
    sin_d = din("sinT", [128, S])
    maskA_d = din("maskA", [128, MASKW], BF16)
    tri_d = din("tri", [128, 128], BF16)
    perm_d = din("perm", [128, 128], BF16)
    identf_d = din("identf", [128, 128])
    identb_d = din("identb", [128, 128], BF16)
    e8_d = din("E8", [8, S], BF16)
    gconst_d = din("gconst", [128, 3, 128])
    sel2_d = din("sel2", [2, 2, 128])
    y_d = nc.dram_tensor("y", [BPC, S, D], F32, kind="ExternalOutput").ap()

    with ExitStack() as st:
        def sb(name, shape, dt):
            return st.enter_context(nc.sbuf_tensor(name, list(shape), dt))

        def pst(name, shape, dt=F32):
            return st.enter_context(nc.psum_tensor(name, list(shape), dt))

        cosT = sb("cosT_sb", [128, S], F32)
        sinT = sb("sinT_sb", [128, S], F32)
        maskA = sb("maskA_sb", [128, MASKW], BF16)
        tri = sb("tri_sb", [128, 128], BF16)
        perm = sb("perm_sb", [128, 128], BF16)
        identf = sb("identf_sb", [128, 128], F32)
        identb = sb("identb_sb", [128, 128], BF16)
        gconst = sb("gconst_sb", [128, 3, 128], F32)
        sel2 = sb("sel2_sb", [2, 2, 128], F32)
        cT = sb("cT_sb", [128, 8, BPC], F32)
        lng = sb("lng_sb", [128, D], F32)
        lnb = sb("lnb_sb", [128, D], F32)
        gate_bc = sb("gate_bc", [128, BPC, D], F32)
        sc = sb("sc_sb", [128, 16, BPC], F32)
        modp = [sb(f"modp{i}", [2, 512], F32) for i in range(2)]
        hT = sb("hT", [128, 8, S], BF16)
        catT = sb("catT", [128, 8, S], BF16)
        xs = [sb(f"xs{i}", [128, 512], F32) for i in range(2)]
        wball = sb("wball", [128, 8192], BF16)
        QT = [[sb(f"QT{i}{h}", [128, S], BF16) for h in range(2)] for i in range(2)]
        KT = [[sb(f"KT{i}{h}", [128, S], BF16) for h in range(2)] for i in range(2)]
        Vt = [sb(f"Vt{i}", [128, 16, 2, 128], BF16) for i in range(2)]
        pbuf = [sb(f"pbuf{i}", [128, 2, 512], BF16) for i in range(3)]
        qb = [sb(f"qb{i}", [128, 512], BF16) for i in range(2)]
        tmpf = sb("tmpf", [128, 2048], F32)
        rcp = [sb(f"rcp{i}", [128, 512], F32) for i in range(2)]
        kms2 = sb("kms2", [128, 2, 8], F32)
        kmhl2 = sb("kmhl2", [128, 2, 16], BF16)
        g16_2 = sb("g16_2", [128, 2, 256], F32)
        gate2 = sb("gate2", [128, 2, 16, 8], F32)
        top8_2 = sb("top8_2", [128, 2, 16, 8], F32)
        selt2 = sb("selt2", [128, 2, 16, 8], F32)
        nmpad2 = sb("nmpad2", [128, 2, 16, 72], BF16)
        epst = sb("epst", [128, 1], F32)
        onet = sb("onet", [128, 1], F32)

        stp = [pst(f"st{i}", [128, 1024]) for i in range(2)]
        otp = [pst(f"ot{i}", [128, 512]) for i in range(2)]
        pp = pst("pp", [128, 512])
        pr = pst("pr", [128, 512])

        wbuf = [wball[:, i * 4096:(i + 1) * 4096].rearrange("p (m s f) -> p m s f", m=8, s=4) for i in range(2)]
        wout = wball[:, :].rearrange("p (m f) -> p m f", m=8)
        t1 = [tmpf[:, 0:512], tmpf[:, 512:1024]]
        t2 = [tmpf[:, 1024:1536], tmpf[:, 1536:2048]]
        rbuf = [tmpf[:, 0:1024], tmpf[:, 1024:2048]]
        sgt = tmpf[:, 0:512]

        S_ = Sched(nc, st)
        op = S_.op

        badaf = wball[0:2, 0:6144].bitcast(F32)
        ev_bada = op("sp", lambda e: e.dma_start(out=badaf, in_=bada_d), sig="ld_bada", dma=True)
        evc = None
        for dst, src in ((cosT, cos_d), (sinT, sin_d), (maskA, maskA_d), (tri, tri_d), (perm, perm_d),
                         (identf, identf_d), (identb, identb_d), (gconst, gconst_d), (sel2, sel2_d),
                         (cT, cT_d), (lng, lng_d), (lnb, lnb_d)):
            evc = op("sp", (lambda d, s: (lambda e: e.dma_start(out=d[:], in_=s)))(dst, src), sig="ld_c", dma=True)
        ev_ms = None
        for i in range(2):
            ev_ms = op("pool", (lambda t: (lambda e: e.memset(t[:], 1.0)))(Vt[i]), sig=True)
        ev_ms = op("pool", lambda e: e.memset(nmpad2[:], 0.0), sig=True)
        ev_ms = op("pool", lambda e: e.memset(epst[:], LN_EPS), sig=True)
        ev_ms = op("pool", lambda e: e.memset(onet[:], 1.0), sig=True)

        xs_free = [None, None]
        modp_free = [None, None]
        ldn = 0
        tr_idx = 0
        ev_pr_free = None
        ev_st_free = [None, None]
        ev_sc = None
        ev_gate = None
        last_tr = None
        wreg = [hT[:, :, :].rearrange("p a b -> p (a b)").bitcast(F32).rearrange("p (m f) -> p m f", m=8),
                catT[:, :, :].rearrange("p a b -> p (a b)").bitcast(F32).rearrange("p (m f) -> p m f", m=8)]
        wreg_free = [None, None]
        wld = {}

        def wada_load(third):
            r = 0 if third == 0 else 1
            src = wada_d[:, third * 1024:(third + 1) * 1024].rearrange("(m p) f -> p m f", p=128)
            wld[third] = op("sp", (lambda d, s_: (lambda e: e.dma_start(out=d, in_=s_)))(wreg[r], src),
                            waits=[wreg_free[r]], sig=f"ld_wa{r}", dma=True)

        wada_load(0)
        wada_load(1)
        Xst = [QT[i][h][:, :].bitcast(F32) for i in range(2) for h in range(2)] + \
              [KT[i][h][:, :].bitcast(F32) for i in range(2) for h in range(2)]
        xst_ld = []
        for idx in range(8):
            mc, half = idx // 2, idx % 2
            xst_ld.append(op("sp", (lambda d, s_: (lambda e: e.dma_start(out=d, in_=s_)))(
                Xst[idx], xT_d[0, mc * 128:(mc + 1) * 128, half * 1024:(half + 1) * 1024]), sig=f"ld_xs{idx}", dma=True))
        for g6 in range(6):
            third, hf = g6 // 2, g6 % 2
            r = 0 if third == 0 else 1
            ev_mm = None
            for mc in range(8):
                last = (mc == 7)
                ev_mm = op("pe", (lambda l, r_, s0, s1: (lambda e: e.matmul(pp[0:2, :], l, r_, start=s0, stop=s1)))(
                    cT[:, mc, :], wreg[r][:, mc, hf * 512:(hf + 1) * 512], mc == 0, last),
                    waits=[wld[third], evc, S_.last("dve")[0] if (mc == 0 and S_.last("dve")) else None],
                    sig=(True if last else None))
            if hf == 1:
                wreg_free[r] = ev_mm
                if third == 1:
                    wada_load(2)
            mi = g6 % 2
            addc = 1.0 if g6 in (2, 3) else 0.0
            ev_mod = op("dve", (lambda m, a, g: (lambda e: e.scalar_tensor_tensor(
                out=m[0:2, :], in0=pp[0:2, :], scalar=a, in1=badaf[0:2, g * 512:(g + 1) * 512],
                op0=ALU.add, op1=ALU.add)))(modp[mi], addc, g6),
                waits=[ev_mm, modp_free[mi], evc, ev_bada], sig=True)
            if g6 < 4:
                for c4 in range(4):
                    idx = g6 * 4 + c4
                    last_tr = op("pe", (lambda m, c, ix: (lambda e: e.matmul(
                        pr[:, ix * 2:(ix + 1) * 2], m[0:2, c * 128:(c + 1) * 128], identf[0:2, 0:2],
                        start=True, stop=True)))(modp[mi], c4, idx),
                        waits=[ev_mod], sig=True)
                modp_free[mi] = last_tr
                if g6 == 3:
                    ev_sc = op("dve", lambda e: e.tensor_copy(
                        sc[:, :, :], pr[:, 0:32].rearrange("p (i b) -> p i b", b=2)),
                        waits=[last_tr], sig=True)
                    ev_pr_free = ev_sc
            else:
                for b in range(BPC):
                    ev_g = op("pe", (lambda m, bb: (lambda e: e.matmul(
                        stp[bb][:, 0:512], sel2[0:2, bb, :], m[0:2, :], start=True, stop=True)))(modp[mi], b),
                        waits=[ev_mod, ev_st_free[b]], sig=True)
                    ev_gate = op("dve", (lambda bb, g: (lambda e: e.tensor_copy(
                        gate_bc[:, bb, (g - 4) * 512:(g - 3) * 512], stp[bb][:, 0:512])))(b, g6),
                        waits=[ev_g], sig=True)
                    ev_st_free[b] = ev_gate
                    modp_free[mi] = ev_g

        state = {
            "pp_free": [ev_sc], "pr_free": [ev_sc],
            "qb_free": [None, None], "t_free": [None, None], "t_free2": [None, None],
            "st_free": [[ev_st_free[0]], [ev_st_free[1]]],
            "ot_free": [[], []], "pb_free": [[], [], []],
            "wb_free": [[], []], "wb_ld": [None, None],
            "qk_free": [[], []], "vt_free": [[], []],
            "nm_free": [],
        }
        setup_done = [ev_sc, ev_gate, ev_ms, evc]

        def unit_cols(u):
            base = 0 if u < 4 else 2048
            uu = u % 4
            return [base + s * 512 + uu * 128 for s in range(4)]

        def load_weights(b, u, extra_waits=()):
            wb = u % 2
            cols = unit_cols(u)
            ev = None
            for s in range(4):
                src = win_d[:, cols[s]:cols[s] + 128].rearrange("(m p) f -> p m f", p=128)
                ev = op("pool", (lambda d, s_: (lambda e: e.dma_start(out=d, in_=s_)))(wbuf[wb][:, :, s, :], src),
                        waits=list(state["wb_free"][wb]) + list(extra_waits), sig=f"ld_w{wb}", dma=True)
            state["wb_ld"][wb] = ev
            state["wb_free"][wb] = []

        def proj_unit(b, u, hT_ready, full=True, silu_u=None):
            wb = u % 2
            qi = u % 2
            is_b = u >= 4
            first_waits = list(state["qk_free"][qi]) + list(state["vt_free"][qi])
            state["qk_free"][qi] = []
            state["vt_free"][qi] = []
            extra_ready = []
            if u in (4, 5):
                extra_ready.append(op("sp", lambda e: e.dma_start(out=KT[qi][0][64:72, :], in_=e8_d),
                                      waits=first_waits, sig=f"ld_e8_u{u}", dma=True))
            Pb = [[stp[0][:, 0:512], list(state["st_free"][0])], [stp[0][:, 512:1024], list(state["st_free"][0])],
                  [stp[1][:, 0:512], list(state["st_free"][1])], [stp[1][:, 512:1024], list(state["st_free"][1])]]
            Rb = [[pr[:, :], list(state["pr_free"])], [pp[:, :], list(state["pp_free"])],
                  [otp[0][:, :], list(state["ot_free"][0])], [otp[1][:, :], list(state["ot_free"][1])]]
            cnt = {"p": 0, "r": 0, "k": 0, "g": 0}
            last_pe = [None]

            def mm_group(bank, s, tok):
                ev = None
                for mc in range(8):
                    ev = op("pe", (lambda o, l, r, s0, s1: (lambda e: e.matmul(o, l, r, start=s0, stop=s1)))(
                        bank[0], wbuf[wb][:, mc, s, :], hT[:, mc, tok], mc == 0, mc == 7),
                        waits=([state["wb_ld"][wb]] + list(hT_ready) + list(bank[1]) if mc == 0 else ()),
                        sig=(True if mc == 7 else None))
                last_pe[0] = ev
                return ev

            def qk_step(s, tt):
                tok = slice(tt * 512, (tt + 1) * 512)
                bank = Pb[cnt["p"] % 4]
                cnt["p"] += 1
                rbank = Rb[cnt["r"] % 4]
                cnt["r"] += 1
                k = cnt["k"] % 2
                cnt["k"] += 1
                dst = (QT if s == 0 else KT)[qi]
                ev = mm_group(bank, s, tok)
                ev_a = op("act", lambda e: e.activation(qb[k][:, :], bank[0], AF.Copy),
                          waits=[ev, state["qb_free"][k]], sig=True)
                ev_t1 = op("dve", lambda e: e.tensor_tensor(t1[k], bank[0], cosT[:, tok], op=ALU.mult),
                           waits=[ev, ev_a, state["t_free"][k], state["t_free2"][k]], sig=True)
                bank[1] = [ev_a, ev_t1]

                def later():
                    ev_p = op("pe", lambda e: e.matmul(rbank[0], perm[:, :], qb[k][:, :], start=True, stop=True),
                              waits=[ev_a] + list(rbank[1]), sig=True)
                    state["qb_free"][k] = ev_p
                    ev_t2 = op("dve", lambda e: e.tensor_tensor(t2[k], rbank[0], sinT[:, tok], op=ALU.mult),
                               waits=[ev_p, ev_t1], sig=True)
                    rbank[1] = [ev_t2]
                    if not is_b:
                        ev_lo = op("dve", lambda e: e.tensor_tensor(dst[0][:, tok], t1[k], t2[k], op=ALU.add),
                                   waits=[ev_t2] + first_waits, sig=True)
                    else:
                        ev_lo = op("pool", lambda e: e.tensor_tensor(dst[0][0:64, tok], t1[k][0:64, :], t2[k][0:64, :], op=ALU.add),
                                   waits=[ev_t2] + first_waits, sig=True)
                        ev_hi = op("dve", lambda e: e.tensor_tensor(dst[1][0:64, tok], t1[k][64:128, :], t2[k][64:128, :], op=ALU.add),
                                   waits=[ev_t2] + first_waits, sig=True)
                        state["t_free2"][k] = ev_hi
                    state["t_free"][k] = ev_lo
                return later

            def g_step(tt):
                tok = slice(tt * 512, (tt + 1) * 512)
                bank = Pb[cnt["p"] % 4]
                cnt["p"] += 1
                gi = cnt["g"] % 2
                cnt["g"] += 1
                sg = rcp[gi][:, :]
                ev = mm_group(bank, 3, tok)

                def later():
                    ea = op("act", lambda e: e.activation(sg, bank[0], AF.Exp, scale=-1.0),
                            waits=[ev] + list(state["rc_free"][gi]), sig=True)
                    ea = op("act", lambda e: e.activation(sg, sg, AF.Ln, bias=onet[:, 0:1]), waits=[ea], sig=True)
                    ea = op("act", lambda e: e.activation(sg, sg, AF.Exp, scale=-1.0), waits=[ea], sig=True)
                    ev_d = op("dve", lambda e: e.tensor_tensor(catT[:, u, tok], bank[0], sg, op=ALU.mult),
                              waits=[ea, ev] + first_waits, sig=True)
                    bank[1] = [ev_d]
                    state["rc_free"][gi] = [ev_d]
                return later

            def v_step(kg):
                rbank = Rb[cnt["r"] % 4]
                cnt["r"] += 1
                ev = None
                for i4 in range(4):
                    kb = kg * 4 + i4
                    for mc in range(8):
                        ev = op("pe", (lambda o, l, r, s0, s1: (lambda e: e.matmul(o, l, r, start=s0, stop=s1)))(
                            rbank[0][:, i4 * 128:(i4 + 1) * 128], hT[:, mc, kb * 128:(kb + 1) * 128], wbuf[wb][:, mc, 2, :],
                            mc == 0, mc == 7),
                            waits=([state["wb_ld"][wb]] + list(hT_ready) + list(rbank[1]) if (mc == 0 and i4 == 0) else ()),
                            sig=(True if (mc == 7 and i4 == 3) else None))
                last_pe[0] = ev

                def later():
                    prv = rbank[0].rearrange("p (i h d) -> p i h d", i=4, h=2)
                    op("act", lambda e: e.activation(Vt[qi][:, kg * 4:(kg + 1) * 4, 0, 0:64], prv[:, :, 0, :], AF.Copy),
                       waits=[ev] + first_waits, sig=True)
                    ev1 = op("act", lambda e: e.activation(Vt[qi][:, kg * 4:(kg + 1) * 4, 1, 64:128], prv[:, :, 1, :], AF.Copy),
                             waits=[ev] + first_waits, sig=True)
                    rbank[1] = [ev1]
                return later

            kmean_ev = [None]

            def kmean_step():
                kts = KT[qi]
                kdone = S_.last("dve", "pool")
                e1 = [op("dve", (lambda h: (lambda e: e.tensor_reduce(
                    out=kms2[0:64, h, :], in_=kts[h][0:64, :].rearrange("p (n k) -> p n k", k=256), axis=AX.X, op=ALU.add)))(hh),
                    waits=kdone + list(state["nm_free"]), sig=True) for hh in range(2)]
                e2 = op("dve", lambda e: e.tensor_scalar(kmhl2[0:64, :, 0:8], kms2[0:64, :, :], 1.0 / 256.0, None, op0=ALU.mult),
                        waits=e1, sig=True)
                kmean_ev[0] = op("dve", lambda e: e.scalar_tensor_tensor(
                    out=kmhl2[0:64, :, 8:16], in0=kms2[0:64, :, :], scalar=1.0 / 256.0, in1=kmhl2[0:64, :, 0:8],
                    op0=ALU.mult, op1=ALU.subtract), waits=[e2], sig=True)
                return None

            def flush_step():
                return None

            def silu_step(tt):
                tok = slice(tt * 512, (tt + 1) * 512)
                gi = cnt["g"] % 2
                cnt["g"] += 1
                sg = rcp[gi][:, :]
                su = silu_u
                ea = op("act", lambda e: e.activation(sg, catT[:, su, tok], AF.Exp, scale=-1.0),
                        waits=list(state["rc_free"][gi]) + list(state["fill_done"]), sig=True)
                ea = op("act", lambda e: e.activation(sg, sg, AF.Ln, bias=onet[:, 0:1]), waits=[ea], sig=True)
                ea = op("act", lambda e: e.activation(sg, sg, AF.Exp, scale=-1.0), waits=[ea], sig=True)
                ev_d = op("dve", lambda e: e.tensor_tensor(catT[:, su, tok], catT[:, su, tok], sg, op=ALU.mult),
                          waits=[ea], sig=True)
                state["rc_free"][gi] = [ev_d]
                return None

            if full:
                steps = [lambda tt=tt: qk_step(1, tt) for tt in range(4)] + [lambda kg=kg: v_step(kg) for kg in range(4)] \
                    + [lambda tt=tt: qk_step(0, tt) for tt in range(4)] + [lambda tt=tt: g_step(tt) for tt in range(4)]
            else:
                steps = []
                for tt in range(4):
                    steps += [lambda tt=tt: qk_step(1, tt)]
                    if silu_u is not None:
                        steps += [lambda tt=tt: silu_step(tt)]
                if silu_u is None:
                    steps += [flush_step]
                nk = len(steps)
                steps += [lambda tt=tt: qk_step(0, tt) for tt in range(4)]
            pending = None
            for stf in steps:
                fin = stf()
                if pending is not None:
                    pending()
                pending = fin
                yield
            if pending is not None:
                pending()
            state["wb_free"][wb] = [last_pe[0]]
            state["pr_free"], state["pp_free"] = list(Rb[0][1]), list(Rb[1][1])
            state["ot_free"] = [list(Rb[2][1]), list(Rb[3][1])]
            state["st_free"] = [list(Pb[0][1]) + list(Pb[1][1]), list(Pb[2][1]) + list(Pb[3][1])]
            state["qk_done"] = S_.last("dve", "pool", "act") + extra_ready
            state["kmean_ev"] = kmean_ev[0]

        def silu_only(b, u):
            for tt in range(4):
                tok = slice(tt * 512, (tt + 1) * 512)
                gi = tt % 2
                sg = rcp[gi][:, :]
                ea = op("act", (lambda sg_, tk: (lambda e: e.activation(sg_, catT[:, u, tk], AF.Exp, scale=-1.0)))(sg, tok),
                        waits=list(state["rc_free"][gi]) + list(state["fill_done"]), sig=True)
                ea = op("act", (lambda sg_: (lambda e: e.activation(sg_, sg_, AF.Ln, bias=onet[:, 0:1])))(sg), waits=[ea], sig=True)
                ea = op("act", (lambda sg_: (lambda e: e.activation(sg_, sg_, AF.Exp, scale=-1.0)))(sg), waits=[ea], sig=True)
                ev_d = op("dve", (lambda sg_, tk: (lambda e: e.tensor_tensor(catT[:, u, tk], catT[:, u, tk], sg_, op=ALU.mult)))(sg, tok),
                          waits=[ea], sig=True)
                state["rc_free"][gi] = [ev_d]

        def gating_gen(b, u):
            qi = u % 2
            rope_done = list(state["qk_done"])
            qts, kts = QT[qi], KT[qi]
            gc = gconst[:, :, :].rearrange("p t (a n) -> p t a n", n=8)
            e1 = []
            for hh in range(2):
                e1.append(op("dve", (lambda h: (lambda e: e.tensor_reduce(
                    out=kms2[0:64, h, :], in_=kts[h][0:64, :].rearrange("p (n k) -> p n k", k=256), axis=AX.X, op=ALU.add)))(hh),
                    waits=rope_done + list(state["nm_free"]), sig=True))
                yield
            e2 = op("dve", lambda e: e.tensor_scalar(kmhl2[0:64, :, 0:8], kms2[0:64, :, :], 1.0 / 256.0, None, op0=ALU.mult),
                    waits=e1, sig=True)
            yield
            e3 = op("dve", lambda e: e.scalar_tensor_tensor(
                out=kmhl2[0:64, :, 8:16], in0=kms2[0:64, :, :], scalar=1.0 / 256.0, in1=kmhl2[0:64, :, 0:8],
                op0=ALU.mult, op1=ALU.subtract), waits=[e2], sig=True)
            yield
            evg = []
            for hh in range(2):
                ev = None
                for q16 in range(16):
                    ev = op("pe", (lambda q, h: (lambda e: e.matmul(
                        pp[:, h * 256 + q * 16:h * 256 + (q + 1) * 16], qts[h][0:64, q * 128:(q + 1) * 128], kmhl2[0:64, h, :],
                        start=True, stop=True)))(q16, hh),
                        waits=([e3] + rope_done + list(state["pp_free"]) if (q16 == 0 and hh == 0) else ()),
                        sig=(True if q16 == 15 else None))
                evg.append(ev)
                yield
            yield
            e4 = [op("dve", lambda e: e.tensor_copy(g16_2[:, :, :], pp[:, :].rearrange("p (h c) -> p h c", h=2)),
                     waits=[evg[1]], sig=True)]
            state["pp_free"] = [e4[0]]
            yield
            g16v = g16_2[:, :, :].rearrange("p h (a c) -> p h a c", c=16)
            e5 = op("dve", lambda e: e.tensor_tensor(gate2[:, :, :, :], g16v[:, :, :, 0:8], g16v[:, :, :, 8:16], op=ALU.add),
                    waits=e4, sig=True)
            yield
            e6 = None
            for hh in range(2):
                e6 = op("dve", (lambda h: (lambda e: e.tensor_tensor(gate2[:, h, :, :], gate2[:, h, :, :], gc[:, 0, :, :], op=ALU.add)))(hh),
                        waits=[e5], sig=True)
            yield
            e7 = None
            for hh in range(2):
                for q16 in range(16):
                    e7 = op("dve", (lambda q, h: (lambda e: e.max(top8_2[:, h, q, :], gate2[:, h, q, :])))(q16, hh),
                            waits=[e6], sig=(True if (q16 == 15 and hh == 1) else None))
                yield
            e8 = op("dve", lambda e: e.tensor_tensor(selt2[:, :, :, :], gate2[:, :, :, :],
                                                     top8_2[:, :, :, 2:3].to_broadcast([128, 2, 16, 8]), op=ALU.is_ge),
                    waits=[e7], sig=True)
            yield
            e9 = None
            for hh in range(2):
                e9 = op("dve", (lambda h: (lambda e: e.tensor_tensor(selt2[:, h, :, :], selt2[:, h, :, :], gc[:, 1, :, :], op=ALU.mult)))(hh),
                        waits=[e8], sig=True)
            yield
            e10 = None
            for hh in range(2):
                e10 = op("dve", (lambda h: (lambda e: e.tensor_tensor(selt2[:, h, :, :], selt2[:, h, :, :], gc[:, 2, :, :], op=ALU.add)))(hh),
                         waits=[e9], sig=True)
            yield
            e11 = op("dve", lambda e: e.tensor_scalar(nmpad2[:, :, :, 64:72], selt2[:, :, :, :], BIG, -BIG,
                                                      op0=ALU.mult, op1=ALU.add),
                     waits=[e10] + list(state["nm_free"]), sig=True)
            yield
            n = 0
            ev = None
            for hh in range(2):
                for piece in range(4):
                    bank = pp
                    fkey = "pp_free"
                    n += 1
                    for i4 in range(4):
                        q16 = piece * 4 + i4
                        ev = op("pe", (lambda q, i_, h, bk: (lambda e: e.matmul(
                            bk[0:72, i_ * 128:(i_ + 1) * 128], nmpad2[:, h, q, 0:72], identb[:, :],
                            start=True, stop=True)))(q16, i4, hh, bank),
                            waits=([e11] + list(state[fkey]) if i4 == 0 else ()),
                            sig=(True if i4 == 3 else None))
                    yield
                    e12 = op("dve", (lambda pc, h, bk: (lambda e: e.tensor_copy(
                        qts[h][64:72, pc * 512:(pc + 1) * 512], bk[64:72, :])))(piece, hh, bank),
                        waits=[ev], sig=True)
                    state[fkey] = [e12]
                    yield
            state["nm_free"] = [ev]

        def vg_filler(b, u, single=False):
            wb = u % 2
            qi = u % 2
            fw = list(state["vt_free"][qi])
            banks = [[pr[:, :], "pr_free"]] if single else [[pr[:, :], "pr_free"], [pp[:, :], "pp_free"]]
            n = 0
            hT_ready = state["hT_ready"]
            for kind, idx in [("v", 0), ("g", 0), ("v", 1), ("g", 1), ("v", 2), ("g", 2), ("v", 3), ("g", 3)]:
                bank, fkey = banks[n % len(banks)]
                n += 1
                ev = None
                if kind == "v":
                    for i4 in range(4):
                        kb = idx * 4 + i4
                        for mc in range(8):
                            ev = op("pe", (lambda o, l, r, s0, s1: (lambda e: e.matmul(o, l, r, start=s0, stop=s1)))(
                                bank[:, i4 * 128:(i4 + 1) * 128], hT[:, mc, kb * 128:(kb + 1) * 128], wbuf[wb][:, mc, 2, :],
                                mc == 0, mc == 7),
                                waits=([state["wb_ld"][wb]] + list(hT_ready) + list(state[fkey]) if (mc == 0 and i4 == 0) else ()),
                                sig=(True if (mc == 7 and i4 == 3) else None))
                        if (not single) or i4 % 2 == 1:
                            yield
                    prv = bank.rearrange("p (i h d) -> p i h d", i=4, h=2)
                    op("dve", (lambda pv, kg: (lambda e: e.tensor_copy(Vt[qi][:, kg * 4:(kg + 1) * 4, 0, 0:64], pv[:, :, 0, :])))(prv, idx),
                       waits=[ev] + fw, sig=True)
                    ev1 = op("dve", (lambda pv, kg: (lambda e: e.tensor_copy(Vt[qi][:, kg * 4:(kg + 1) * 4, 1, 64:128], pv[:, :, 1, :])))(prv, idx),
                             waits=[ev] + fw, sig=True)
                    state[fkey] = [ev1]
                else:
                    tok = slice(idx * 512, (idx + 1) * 512)
                    for mc in range(8):
                        ev = op("pe", (lambda o, l, r, s0, s1: (lambda e: e.matmul(o, l, r, start=s0, stop=s1)))(
                            bank, wbuf[wb][:, mc, 3, :], hT[:, mc, tok], mc == 0, mc == 7),
                            waits=([state["wb_ld"][wb]] + list(hT_ready) + list(state[fkey]) if mc == 0 else ()),
                            sig=(True if mc == 7 else None))
                        if (mc % 2 == 1 and not single) or (mc % 4 == 3 and single):
                            yield
                    ev1 = op("dve", (lambda bk, tk: (lambda e: e.tensor_copy(catT[:, u, tk], bk)))(bank, tok),
                             waits=[ev], sig=True)
                    state[fkey] = [ev1]
                state["fill_done"] = [ev1]
                state["fill_last_pe"] = [ev]
                yield
                if single:
                    yield

        def att_unit(b, u, ready, qready=None):
            if qready is None:
                qready = ready
            qi = u % 2
            is_b = u >= 4
            K = 72 if is_b else 64
            groups = []
            for hh in range(2):
                for qt in range(4):
                    ng = 2 * qt + 2
                    for j in range(ng):
                        groups.append((hh, qt, j, j == ng - 1))
            n = len(groups)
            exp_ev = [None] * n
            msk_ev = [None] * n
            ot_of = {}
            last_norm = []
            for i in range(n + 2):
                if i < n:
                    hh, qt, j, lastg = groups[i]
                    sti = state["gcount"] % 2
                    pbi = state["gcount"] % 3
                    state["gcount"] += 1
                    if is_b:
                        qsl, ksl, rows = QT[qi][hh], KT[qi][hh], slice(0, K)
                    else:
                        qsl, ksl, rows = QT[qi][0], KT[qi][0], slice(64 * hh, 64 * hh + 64)
                    kbs = (2 * j, 2 * j + 1)
                    c0s = [max(0, (kb - 4 * qt) * 128) for kb in kbs]
                    ev = None
                    for jj in range(2):
                        kb, c0 = kbs[jj], c0s[jj]
                        ev = op("pe", (lambda o, l, r: (lambda e: e.matmul(o, l, r, start=True, stop=True)))(
                            stp[sti][:, jj * 512 + c0:(jj + 1) * 512], ksl[rows, kb * 128:(kb + 1) * 128],
                            qsl[rows, qt * 512 + c0:(qt + 1) * 512]),
                            waits=(list(qready) + list(state["st_free"][sti]) if jj == 0 else ()),
                            sig=(True if jj == 1 else None))
                    stv = stp[sti][:, :].rearrange("p (j c) -> p j c", j=2)
                    if c0s[0] == c0s[1]:
                        ev_e = op("act", (lambda pb_, sv, c: (lambda e: e.activation(
                            pb_[:, :, c:512], sv[:, :, c:512], AF.Exp, scale=0.125)))(pbuf[pbi], stv, c0s[0]),
                            waits=[ev] + list(state["pb_free"][pbi]), sig=True)
                    else:
                        for jj in range(2):
                            ev_e = op("act", (lambda pb_, sv, c, j_: (lambda e: e.activation(
                                pb_[:, j_, c:512], sv[:, j_, c:512], AF.Exp, scale=0.125)))(pbuf[pbi], stv, c0s[jj], jj),
                                waits=[ev] + list(state["pb_free"][pbi]), sig=True)
                    state["st_free"][sti] = [ev_e]
                    exp_ev[i] = (ev_e, pbi, c0s)
                    evs_m = []
                    for jj in range(2):
                        kb, c0 = kbs[jj], c0s[jj]
                        if not is_b:
                            o_ = 4 * qt - kb
                            base = 128 * (o_ + 3)
                            eng = "dve"
                            evs_m.append(op(eng, (lambda pb_, j_, c, bs: (lambda e: e.tensor_tensor(
                                pb_[:, j_, c:512], pb_[:, j_, c:512], maskA[:, bs + c:bs + 512], op=ALU.mult)))(
                                pbuf[pbi], jj, c0, base), waits=[ev_e], sig=True))
                        elif kb >= 4 * qt:
                            evs_m.append(op("dve", (lambda pb_, j_, c: (lambda e: e.tensor_tensor(
                                pb_[:, j_, c:c + 128], pb_[:, j_, c:c + 128], tri[:, :], op=ALU.mult)))(
                                pbuf[pbi], jj, c0), waits=[ev_e], sig=True))
                    msk_ev[i] = evs_m
                if i >= 2:
                    i2 = i - 2
                    hh, qt, j, lastg = groups[i2]
                    ev_e, pbi, c0s = exp_ev[i2]
                    key = (hh, qt)
                    if key not in ot_of:
                        ot_of[key] = state["ot_n"] % 2
                        state["ot_n"] += 1
                    oi = ot_of[key]
                    ev = None
                    for jj in range(2):
                        kb, c0 = 2 * j + jj, c0s[jj]
                        first = (j == 0 and jj == 0)
                        ev = op("pe", (lambda o, l, r, f: (lambda e: e.matmul(o, l, r, start=f, stop=False,
                                                                              skip_group_check=True)))(
                            otp[oi][:, c0:512], Vt[qi][:, kb, hh, :], pbuf[pbi][:, jj, c0:512], first),
                            waits=(list(msk_ev[i2]) + [ev_e] + (list(state["ot_free"][oi]) if first else []) if jj == 0 else ()),
                            sig=(True if jj == 1 else None))
                    state["pb_free"][pbi] = [ev]
                    if lastg:
                        tok = slice(qt * 512, (qt + 1) * 512)
                        ri = state["rc_n"] % 2
                        state["rc_n"] += 1
                        orow = slice(0, 64) if hh == 0 else slice(64, 128)
                        lrow = slice(64, 128) if hh == 0 else slice(0, 64)
                        e1a = op("act", (lambda r_, o_, orow_, lrow_: (lambda e: e.activation(r_[orow_, :], o_[lrow_, :], AF.Ln)))(
                            rcp[ri], otp[oi], orow, lrow), waits=[ev] + list(state["rc_free"][ri]), sig=True)
                        e1 = op("act", (lambda r_, orow_: (lambda e: e.activation(r_[orow_, :], r_[orow_, :], AF.Exp, scale=-1.0)))(
                            rcp[ri], orow), waits=[e1a], sig=True)
                        e2 = op("dve", (lambda r_, o_, orow_: (lambda e: e.tensor_tensor(
                            r_[orow_, :], o_[orow_, :], r_[orow_, :], op=ALU.mult)))(rcp[ri], otp[oi], orow),
                            waits=[e1], sig=True)
                        state["ot_free"][oi] = [e2]
                        e3 = op("pool", (lambda n_, orow_, tk: (lambda e: e.tensor_tensor(
                            catT[orow_, u, tk], n_[orow_, :], catT[orow_, u, tk], op=ALU.mult)))(rcp[ri], orow, tok),
                            waits=[e2] + list(ready), sig=True)
                        state["rc_free"][ri] = [e3]
                        last_norm = [e3]
                yield
            state["qk_free"][qi] = S_.last("pe")
            state["vt_free"][qi] = S_.last("pe")
            state["att_done"] = last_norm

        state["gcount"] = 0
        state["ot_n"] = 0
        state["rc_n"] = 0
        state["rc_free"] = [[], []]

        def run(gen):
            for _ in gen:
                pass

        def merge3(ga, gb, gc_):
            done_b = gb is None
            done_c = gc_ is None
            for _ in ga:
                if not done_b:
                    try:
                        next(gb)
                    except StopIteration:
                        done_b = True
                if not done_c:
                    try:
                        next(gc_)
                    except StopIteration:
                        done_c = True
            if not done_b:
                for _ in gb:
                    pass
            if not done_c:
                for _ in gc_:
                    pass

        def merge(ga, gb, ratio):
            acc = 0.0
            gb_done = gb is None
            for _ in ga:
                acc += ratio
                while acc >= 1.0 and not gb_done:
                    acc -= 1.0
                    try:
                        next(gb)
                    except StopIteration:
                        gb_done = True
            if not gb_done:
                for _ in gb:
                    pass

        ev_out = None
        stats4 = [sb(f"stats4_{i}", [128, 2, 6], F32) for i in range(4)]
        mv4 = [sb(f"mv4_{i}", [128, 2], F32) for i in range(4)]
        rstd4 = [sb(f"rstd4_{i}", [128, 1], F32) for i in range(4)]
        nb4 = [sb(f"nb4_{i}", [128, 1], F32) for i in range(4)]
        Xb = [QT[i][h][:, :].bitcast(F32) for i in range(2) for h in range(2)]
        Rb4 = [KT[i][h][:, :].bitcast(F32) for i in range(2) for h in range(2)]
        fin_free = [[], [], [], []]
        fin_pe_done = []

        def hT_chunk(bn, idx, waits0):
            mc, q4 = idx // 4, idx % 4
            i = idx % 2
            src = xT_d[bn, mc * 128:(mc + 1) * 128, q4 * 512:(q4 + 1) * 512]
            evl = op("sp", (lambda d, s_: (lambda e: e.dma_start(out=d[:, :], in_=s_)))(xs[i], src),
                     waits=[xs_free[i]] + list(waits0), sig=f"ld_x{i}", dma=True)
            eva = op("act", (lambda d, mc_, h_, b_: (lambda e: e.activation(
                hT[:, mc_, h_ * 512:(h_ + 1) * 512], d[:, :], AF.Identity,
                bias=sc[:, mc_, b_:b_ + 1], scale=sc[:, 8 + mc_, b_:b_ + 1])))(xs[i], mc, q4, bn),
                waits=[evl] + list(waits0), sig=True)
            xs_free[i] = eva
            return eva

        hT_ready = []
        xst_free = [None] * 8
        for b in range(BPC):
            bar = S_.last("pe", "act", "dve", "pool") + (setup_done if b == 0 else [])
            if b == 0:
                for idx in range(16):
                    mc, half = idx // 2, idx % 2
                    sidx = idx % 8
                    if idx >= 8:
                        xst_ld[sidx] = op("sp", (lambda d, s_: (lambda e: e.dma_start(out=d, in_=s_)))(
                            Xst[sidx], xT_d[0, mc * 128:(mc + 1) * 128, half * 1024:(half + 1) * 1024]),
                            waits=[xst_free[sidx]], sig=f"ld_xs{sidx}", dma=True)
                    eva = op("act", (lambda d, mc_, h_: (lambda e: e.activation(
                        hT[:, mc_, h_ * 1024:(h_ + 1) * 1024], d, AF.Identity,
                        bias=sc[:, mc_, 0:1], scale=sc[:, 8 + mc_, 0:1])))(Xst[sidx], mc, half),
                        waits=[xst_ld[sidx], ev_sc], sig=True)
                    xst_free[sidx] = eva
                    hT_ready = [eva]
                for i in range(2):
                    for h in range(2):
                        extra = op("sp", (lambda d: (lambda e: e.dma_start(out=d[64:72, :], in_=e8_d)))(KT[i][h]),
                                   waits=hT_ready, sig="ld_e8", dma=True)
                bar = bar + [extra]
                load_weights(b, 0, extra_waits=bar)
                load_weights(b, 1, extra_waits=bar)
                state["qk_free"] = [list(bar), list(bar)]
            else:
                qf = []
                for i in range(2):
                    fr = list(fin_free[2 * i]) + list(fin_free[2 * i + 1]) + list(fin_pe_done)
                    ex = op("sp", (lambda d: (lambda e: e.dma_start(out=d[64:72, :], in_=e8_d)))(KT[i][1]),
                            waits=fr, sig=f"ld_e8b{i}", dma=True)
                    qf.append(fr + [ex])
                state["qk_free"] = qf
            state["hT_ready"] = hT_ready
            state["fill_done"] = []
            def e8_events():
                return [ev_ for ev_ in state["qk_done"] if ev_[0].startswith("ld_e8")]

            e8_all = []
            run(proj_unit(b, 0, hT_ready, full=True))
            load_weights(b, 2)
            qsnap = S_.last("dve", "pool", "act") + e8_all
            run(proj_unit(b, 1, hT_ready, full=False, silu_u=None))
            for u in range(8):
                ready = S_.last("dve", "pool", "act") + e8_all
                if u < 7:
                    gg = gating_gen(b, u + 1) if u + 1 >= 4 else None
                    merge3(att_unit(b, u, ready, qsnap), vg_filler(b, u + 1, single=(gg is not None)), gg)
                    qsnap = S_.last("dve", "pool", "act") + e8_all
                    state["wb_free"][(u + 1) % 2] = list(state["fill_last_pe"])
                    if u + 3 < 8:
                        load_weights(b, u + 3)
                    if u + 2 < 8:
                        run(proj_unit(b, u + 2, hT_ready, full=False, silu_u=u + 1))
                        e8_all += e8_events()
                    else:
                        evw = None
                        for half in range(2):
                            src = wout_d[:, half * 512:(half + 1) * 512].rearrange("(m p) f -> p m f", p=128)
                            evw = op("pool", (lambda s_, h_: (lambda e: e.dma_start(out=wout[:, :, h_ * 512:(h_ + 1) * 512], in_=s_)))(src, half),
                                     waits=list(state["wb_free"][0]) + list(state["wb_free"][1]), sig="ld_wo", dma=True)
                        silu_only(b, u + 1)
                else:
                    for i_att, _ in enumerate(att_unit(b, u, ready, qsnap)):
                        if i_att == 22:
                            for hf in range(2):
                                ev_wg = op("dve", (lambda h_, b_: (lambda e: e.tensor_tensor(
                                    wout[:, 4 * h_:4 * h_ + 4, :], wout[:, 4 * h_:4 * h_ + 4, :],
                                    gate_bc[:, b_:b_ + 1, :].to_broadcast([128, 4, D]), op=ALU.mult)))(hf, b),
                                    waits=[evw], sig=True)
            fbar = S_.last("pe", "act", "dve", "pool")
            hT_next = []
            stage1 = {}
            stage2 = {}

            def fin_s0(t16):
                j = t16 % 4
                si = t16 % 2
                tok = slice(t16 * 128, (t16 + 1) * 128)
                evl = op("sp", (lambda d, s_: (lambda e: e.dma_start(out=d, in_=s_)))(Xb[j], x_d[b, tok, :]),
                         waits=list(fin_free[j]) + fbar, sig=f"ld_f{j}", dma=True)
                ev = None
                for half in range(2):
                    for uu in range(8):
                        ev = op("pe", (lambda o, l, r, s0, s1: (lambda e: e.matmul(o, l, r, start=s0, stop=s1)))(
                            stp[si][:, half * 512:(half + 1) * 512], catT[:, uu, tok], wout[:, uu, half * 512:(half + 1) * 512],
                            uu == 0, uu == 7),
                            waits=([evw, ev_wg] + fbar + list(state["st_free"][si]) if (uu == 0 and half == 0) else ()),
                            sig=(True if (uu == 7 and half == 1) else None))
                for _d in range(N_WARM):
                    op("pe", lambda e: e.matmul(pp[:, :], identb[:, :], wout[:, 0, 0:512], start=True, stop=True))
                return (evl, ev)

            def fin_s1(t16, evs):
                j = t16 % 4
                si = t16 % 2
                evl, ev = evs
                e2 = op("dve", (lambda j_, s_: (lambda e: e.scalar_tensor_tensor(
                    out=Xb[j_], in0=Xb[j_], scalar=ALPHA, in1=stp[s_][:, :], op0=ALU.mult, op1=ALU.add)))(j, si),
                    waits=[ev, evl] + fbar, sig=True)
                state["st_free"][si] = [e2]
                e3 = None
                for hf in range(2):
                    e3 = op("dve", (lambda j_, h_: (lambda e: e.bn_stats(stats4[j_][:, h_, :], Xb[j_][:, h_ * 512:(h_ + 1) * 512])))(j, hf),
                            waits=[e2], sig=(True if hf == 1 else None))
                e4 = op("dve", (lambda j_: (lambda e: e.bn_aggr(mv4[j_][:, :], stats4[j_][:, :, :])))(j), waits=[e3], sig=True)
                e5a = op("act", (lambda j_: (lambda e: e.activation(rstd4[j_][:, :], mv4[j_][:, 1:2], AF.Ln, bias=epst[:, 0:1])))(j),
                         waits=[e4], sig=True)
                e5 = op("act", (lambda j_: (lambda e: e.activation(rstd4[j_][:, :], rstd4[j_][:, :], AF.Exp, scale=-0.5)))(j),
                        waits=[e5a], sig=True)
                return e5

            def fin_s2(t16, e5):
                j = t16 % 4
                tok = slice(t16 * 128, (t16 + 1) * 128)
                e6n = op("dve", (lambda j_: (lambda e: e.scalar_tensor_tensor(
                    out=nb4[j_][:, :], in0=mv4[j_][:, 0:1], scalar=-1.0, in1=rstd4[j_][:, :], op0=ALU.mult, op1=ALU.mult)))(j),
                    waits=[e5], sig=True)
                e6 = op("act", (lambda j_: (lambda e: e.activation(Rb4[j_], Xb[j_], AF.Identity,
                                                                   bias=nb4[j_][:, 0:1], scale=rstd4[j_][:, 0:1])))(j),
                        waits=[e6n, e5] + list(fin_free[j]), sig=True)
                e8 = op("pool", (lambda j_: (lambda e: e.tensor_tensor(Rb4[j_], Rb4[j_], lng[:, :], op=ALU.mult)))(j),
                        waits=[e6], sig=True)
                return e8

            def fin_s3(t16, e8):
                j = t16 % 4
                tok = slice(t16 * 128, (t16 + 1) * 128)
                e9 = op("dve", (lambda j_: (lambda e: e.tensor_tensor(Xb[j_], Rb4[j_], lnb[:, :], op=ALU.add)))(j),
                        waits=[e8], sig=True)
                evo = op("sp", (lambda s_, j_: (lambda e: e.dma_start(out=s_, in_=Xb[j_])))(y_d[b, tok, :], j),
                         waits=[e9], sig=f"st_y{j}", dma=True)
                fin_free[j] = [evo]
                return evo

            stage3 = {}
            for it in range(16 + 3):
                if it < 16:
                    stage1[it] = fin_s0(it)
                if 1 <= it <= 16:
                    stage2[it - 1] = fin_s1(it - 1, stage1[it - 1])
                if 2 <= it <= 17:
                    stage3[it - 2] = fin_s2(it - 2, stage2[it - 2])
                if it >= 3:
                    ev_out = fin_s3(it - 3, stage3[it - 3])
                if b + 1 < BPC and it < 16:
                    hT_chunk(b + 1, 2 * it, fbar)
                    hT_next = [hT_chunk(b + 1, 2 * it + 1, fbar)]
                if it == 15:
                    fin_pe_done = S_.last("pe")
                    state["wb_free"] = [list(fin_pe_done), list(fin_pe_done)]
                    if b + 1 < BPC:
                        load_weights(b + 1, 0)
                        load_weights(b + 1, 1)
            hT_ready = hT_next
        fin = S_.last("st_y0", "st_y1", "st_y2", "st_y3")
        op("sp", lambda e: e.nop(), waits=fin)

        with nc.Block() as block:
            @block.tensor
            def _(e):
                S_.replay("pe", e)

            @block.scalar
            def _(e):
                S_.replay("act", e)

            @block.vector
            def _(e):
                S_.replay("dve", e)

            @block.gpsimd
            def _(e):
                S_.replay("pool", e)

            @block.sync
            def _(e):
                S_.replay("sp", e)
    return nc


def _consts():
    f32 = np.float32
    half = 32
    inv = np.power(f32(10000.0), -(np.arange(half, dtype=f32) / f32(half))).astype(f32)
    pos = np.arange(S, dtype=f32)
    ang = (pos[:, None] * inv[None, :]).astype(f32)
    cos = np.cos(ang).astype(f32).T
    sin = np.sin(ang).astype(f32).T
    cos64 = np.concatenate([cos, cos], 0)
    sin64 = np.concatenate([-sin, sin], 0)
    cosT = np.ascontiguousarray(np.concatenate([cos64, cos64], 0))
    sinT = np.ascontiguousarray(np.concatenate([sin64, sin64], 0))
    c = np.arange(MASKW)[None, :]
    p = np.arange(128)[:, None]
    dl = c - 384 - p
    m = ((dl >= 0) & (dl <= 128)).astype(f32) + ((dl >= 0) & (dl % 4 == 0) & (dl <= 512)).astype(f32) \
        + ((dl >= 0) & (dl % 16 == 0) & (dl <= 2048)).astype(f32)
    maskA = m.astype(ml_dtypes.bfloat16)
    i = np.arange(128)[None, :]
    tri = (i >= p).astype(f32).astype(ml_dtypes.bfloat16)
    perm = np.zeros((128, 128), f32)
    for mm in range(128):
        partner = mm + 32 if (mm % 64) < 32 else mm - 32
        perm[partner, mm] = 1.0
    perm = perm.astype(ml_dtypes.bfloat16)
    identf = np.eye(128, dtype=f32)
    identb = np.eye(128, dtype=f32).astype(ml_dtypes.bfloat16)
    E8 = np.zeros((8, S), f32)
    for nn in range(8):
        E8[nn, nn * 256:(nn + 1) * 256] = 1.0
    E8 = E8.astype(ml_dtypes.bfloat16)
    gconst = np.zeros((128, 3, 16, 8), f32)
    for q16 in range(16):
        own = q16 // 2
        for nn in range(8):
            past = nn < own
            gconst[:, 0, q16, nn] = 0.0 if past else -1e30
            gconst[:, 1, q16, nn] = 1.0 if past else 0.0
            gconst[:, 2, q16, nn] = 1.0 if nn == own else 0.0
    gconst = gconst.reshape(128, 3, 128)
    sel2 = np.zeros((2, 2, 128), f32)
    sel2[0, 0, :] = 1.0
    sel2[1, 1, :] = 1.0
    return dict(cosT=cosT, sinT=sinT, maskA=maskA, tri=tri, perm=perm, identf=identf, identb=identb,
                E8=E8, gconst=gconst, sel2=sel2)


def kernel(x, c, w_in, w_out, w_ada, b_ada, ln_g, ln_b):
    x = np.asarray(x, dtype=np.float32)
    c = np.asarray(c, dtype=np.float32)
    w_in = np.ascontiguousarray(np.asarray(w_in, dtype=np.float32)[0])
    w_out = np.ascontiguousarray(np.asarray(w_out, dtype=np.float32)[0])
    w_ada = np.ascontiguousarray(np.asarray(w_ada, dtype=np.float32)[0])
    b_ada = np.asarray(b_ada, dtype=np.float32)[0]
    ln_g = np.asarray(ln_g, dtype=np.float32)[0]
    ln_b = np.asarray(ln_b, dtype=np.float32)[0]
    consts = _consts()
    shared = dict(consts)
    shared.update(
        w_in=w_in, w_out=w_out, w_ada=w_ada,
        bada2=np.ascontiguousarray(np.broadcast_to(b_ada[None, :], (2, 3072))),
        lng=np.ascontiguousarray(np.broadcast_to(ln_g[None, :], (128, D))),
        lnb=np.ascontiguousarray(np.broadcast_to(ln_b[None, :], (128, D))),
    )
    in_maps = []
    for k in range(NCORES):
        xb = x[k * BPC:(k + 1) * BPC]
        cb = c[k * BPC:(k + 1) * BPC]
        cT = np.ascontiguousarray(cb.reshape(BPC, 8, 128).transpose(2, 1, 0))
        m = dict(shared)
        m["x"] = np.ascontiguousarray(xb)
        m["xT"] = np.ascontiguousarray(xb.transpose(0, 2, 1))
        m["cT"] = cT
        in_maps.append(m)
    nc = build_program()
    res = run_bass_kernel_spmd(nc, in_maps, core_ids=list(range(NCORES)))
    out = np.concatenate([np.asarray(r["y"], dtype=np.float32) for r in res.results], axis=0)
    return out
```
